# Optimizing a Trainium2 kernel written in Bass

```python
import math
import jax, jax.numpy as jnp
from jax import lax
import numpy as np

D_MODEL = 1024
BATCH = 2
SEQ = 16384
DEPTH = 4
DEC_BATCH = 32
DEC_SEQ = 2048
PAST_LEN = 128

N_MIXERS = 2
N_ATTN_LAYERS = (DEPTH + 1) // 2
N_MLSTM_LAYERS = DEPTH // 2

ATTN_HEADS = 16
ATTN_KV_HEADS = 4
ATTN_GROUP = ATTN_HEADS // ATTN_KV_HEADS
ATTN_HEAD_DIM = D_MODEL // ATTN_HEADS
ATTN_WIDTH = ATTN_HEADS * ATTN_HEAD_DIM
ATTN_KV_WIDTH = ATTN_KV_HEADS * ATTN_HEAD_DIM
ATTN_IN = 2 * ATTN_WIDTH + 2 * ATTN_KV_WIDTH
WINDOW = 128
BLOCK = 128
ROPE_THETA = 10000.0

MLSTM_HEADS = 4
MLSTM_V_DIM = D_MODEL // MLSTM_HEADS
MLSTM_QK_DIM = MLSTM_V_DIM // 2
MLSTM_WIDTH = MLSTM_HEADS * MLSTM_V_DIM
MLSTM_QK_WIDTH = MLSTM_HEADS * MLSTM_QK_DIM
MLSTM_N_GATES = 4 * MLSTM_HEADS
MLSTM_IN = 2 * MLSTM_QK_WIDTH + 3 * MLSTM_WIDTH + MLSTM_N_GATES
CHUNK = 64

EPS = 1e-6
NEG_INIT = -1e30

kernel_name = 'hybrid_swa_mlstm_bidir_encoder'


def rms_norm(x, g):
    xf = x.astype(jnp.float32)
    y = xf * lax.rsqrt(jnp.mean(xf * xf, axis=-1, keepdims=True) + EPS)
    return (y * g.astype(jnp.float32)).astype(x.dtype)


def rope(x, pos):
    half = x.shape[-1] // 2
    inv = jnp.exp(-math.log(ROPE_THETA) * jnp.arange(half, dtype=jnp.float32) / half)
    ang = pos.astype(jnp.float32)[:, None] * inv[None, :]
    cos = jnp.cos(ang)[None, :, None, :]
    sin = jnp.sin(ang)[None, :, None, :]
    xf = x.astype(jnp.float32)
    x1, x2 = xf[..., :half], xf[..., half:]
    return jnp.concatenate([x1 * cos - x2 * sin, x2 * cos + x1 * sin], axis=-1).astype(x.dtype)


def window_attention_mixer(xn, w_in, sink, w_out):
    B, S, _ = xn.shape
    NB = S // BLOCK
    proj = xn @ w_in
    q, k, v, z = jnp.split(proj, [ATTN_WIDTH, ATTN_WIDTH + ATTN_KV_WIDTH, ATTN_WIDTH + 2 * ATTN_KV_WIDTH], axis=-1)
    pos = jnp.arange(S)
    q = rope(q.reshape(B, S, ATTN_HEADS, ATTN_HEAD_DIM), pos)
    k = rope(k.reshape(B, S, ATTN_KV_HEADS, ATTN_HEAD_DIM), pos)
    v = v.reshape(B, S, ATTN_KV_HEADS, ATTN_HEAD_DIM)
    qb = q.reshape(B, NB, BLOCK, ATTN_KV_HEADS, ATTN_GROUP, ATTN_HEAD_DIM)

    def band(t):
        tp = jnp.pad(t, ((0, 0), (BLOCK, BLOCK), (0, 0), (0, 0)))
        tp = tp.reshape(B, NB + 2, BLOCK, ATTN_KV_HEADS, ATTN_HEAD_DIM)
        return jnp.concatenate([tp[:, :-2], tp[:, 1:-1], tp[:, 2:]], axis=2)

    kb, vb = band(k), band(v)
    s = jnp.einsum('bnqgrd,bnkgd->bngrqk', qb, kb, preferred_element_type=jnp.float32)
    s = s * (ATTN_HEAD_DIM ** -0.5)
    a = jnp.arange(BLOCK)
    c = jnp.arange(3 * BLOCK)
    in_band = jnp.abs(c[None, :] - BLOCK - a[:, None]) <= WINDOW
    key_pos = jnp.arange(NB)[:, None] * BLOCK - BLOCK + c[None, :]
    in_seq = (key_pos >= 0) & (key_pos < S)
    mask = in_band[None, :, :] & in_seq[:, None, :]
    s = jnp.where(mask[None, :, None, None], s, -jnp.inf)
    sk = sink.astype(jnp.float32).reshape(ATTN_KV_HEADS, ATTN_GROUP)[None, None, :, :, None, None]
    m = jnp.maximum(jnp.max(s, axis=-1, keepdims=True), sk)
    p = jnp.exp(s - m)
    den = jnp.sum(p, axis=-1, keepdims=True) + jnp.exp(sk - m)
    p = (p / den).astype(vb.dtype)
    o = jnp.einsum('bngrqk,bnkgd->bnqgrd', p, vb).reshape(B, S, ATTN_WIDTH)
    return (o * jax.nn.silu(z)) @ w_out


def mlstm_chunkwise(q, k, v, ig, fg):
    B, S, H, DQK = q.shape
    DV = v.shape[-1]
    NC = S // CHUNK

    def chunks(t):
        return t.astype(jnp.float32).reshape(B, NC, CHUNK, H, -1).transpose(0, 3, 1, 2, 4)

    qc, kc, vc = chunks(q), chunks(k), chunks(v)
    igc = ig.reshape(B, NC, CHUNK, H).transpose(0, 3, 1, 2)
    logf = jax.nn.log_sigmoid(fg.reshape(B, NC, CHUNK, H).transpose(0, 3, 1, 2))
    b = jnp.cumsum(logf, axis=-1)
    b_end = b[..., -1]
    tri = jnp.tril(jnp.ones((CHUNK, CHUNK), dtype=bool))
    D = jnp.where(tri, b[..., :, None] - b[..., None, :] + igc[..., None, :], -jnp.inf)

    a = b_end[..., None] - b + igc
    m_loc = jnp.max(a, axis=-1)
    w = jnp.exp(a - m_loc[..., None])
    C_loc = jnp.einsum('bhcsk,bhcsv->bhckv', w[..., None] * kc, vc)
    n_loc = jnp.einsum('bhcs,bhcsk->bhck', w, kc)

    def step(carry, inp):
        C, n, m = carry
        C_l, n_l, m_l, be = inp
        m_new = jnp.maximum(be + m, m_l)
        sp = jnp.exp(be + m - m_new)
        sl = jnp.exp(m_l - m_new)
        C_new = sp[..., None, None] * C + sl[..., None, None] * C_l
        n_new = sp[..., None] * n + sl[..., None] * n_l
        return (C_new, n_new, m_new), (C, n, m)

    xs = (jnp.moveaxis(C_loc, 2, 0), jnp.moveaxis(n_loc, 2, 0), jnp.moveaxis(m_loc, 2, 0), jnp.moveaxis(b_end, 2, 0))
    init = (jnp.zeros((B, H, DQK, DV), jnp.float32), jnp.zeros((B, H, DQK), jnp.float32),
            jnp.full((B, H), NEG_INIT, jnp.float32))
    _, (Cs, ns, ms) = lax.scan(step, init, xs)
    Cs = jnp.moveaxis(Cs, 0, 2)
    ns = jnp.moveaxis(ns, 0, 2)
    ms = jnp.moveaxis(ms, 0, 2)

    g_prev = b + ms[..., None]
    m_t = jnp.maximum(jnp.max(D, axis=-1), g_prev)
    P = jnp.exp(D - m_t[..., None]) * jnp.einsum('bhctk,bhcsk->bhcts', qc, kc)
    sc = jnp.exp(g_prev - m_t)
    num = jnp.einsum('bhcts,bhcsv->bhctv', P, vc) + sc[..., None] * jnp.einsum('bhctk,bhckv->bhctv', qc, Cs)
    den = jnp.sum(P, axis=-1) + sc * jnp.einsum('bhctk,bhck->bhct', qc, ns)
    h = num / jnp.maximum(jnp.abs(den), jnp.exp(-m_t))[..., None]
    return h.transpose(0, 2, 3, 1, 4).reshape(B, S, H, DV)


def mlstm_mixer(xn, w_in, gate_bias, head_norm, w_out):
    B, S, _ = xn.shape
    cuts = [MLSTM_QK_WIDTH, 2 * MLSTM_QK_WIDTH, 2 * MLSTM_QK_WIDTH + MLSTM_WIDTH,
            2 * MLSTM_QK_WIDTH + 2 * MLSTM_WIDTH, 2 * MLSTM_QK_WIDTH + 3 * MLSTM_WIDTH]
    proj = xn @ w_in
    q, k, v, o, z, g = jnp.split(proj, cuts, axis=-1)
    q = q.reshape(B, S, MLSTM_HEADS, MLSTM_QK_DIM)
    k = k.reshape(B, S, MLSTM_HEADS, MLSTM_QK_DIM) * (MLSTM_QK_DIM ** -0.5)
    v = v.reshape(B, S, MLSTM_HEADS, MLSTM_V_DIM)
    g = (g.astype(jnp.float32) + gate_bias.astype(jnp.float32)).reshape(B, S, 4, MLSTM_HEADS)
    ig_f, fg_f, ig_b, fg_b = g[:, :, 0], g[:, :, 1], g[:, :, 2], g[:, :, 3]
    h_f = mlstm_chunkwise(q, k, v, ig_f, fg_f)
    flip = lambda t: jnp.flip(t, axis=1)
    h_b = flip(mlstm_chunkwise(flip(q), flip(k), flip(v), flip(ig_b), flip(fg_b)))
    h = h_f + h_b
    h = h * lax.rsqrt(jnp.mean(h * h, axis=-1, keepdims=True) + EPS)
    h = h * head_norm.astype(jnp.float32).reshape(MLSTM_HEADS, MLSTM_V_DIM)
    h = h.reshape(B, S, MLSTM_WIDTH).astype(xn.dtype)
    h = h * jax.nn.sigmoid(o) * jax.nn.silu(z)
    return h @ w_out


def trunk(x, norm_g, attn_w_in, attn_sink, attn_w_out, mlstm_w_in, mlstm_gate_bias, mlstm_head_norm, mlstm_w_out, final_norm_g):
    for i in range(DEPTH):
        j = i // N_MIXERS
        xn = rms_norm(x, norm_g[i])
        if i % N_MIXERS == 0:
            x = x + window_attention_mixer(xn, attn_w_in[j], attn_sink[j], attn_w_out[j])
        else:
            x = x + mlstm_mixer(xn, mlstm_w_in[j], mlstm_gate_bias[j], mlstm_head_norm[j], mlstm_w_out[j])
    return rms_norm(x, final_norm_g)


def setup_inputs(seed: int = 0) -> dict:
    key = jax.random.key(seed)
    ks = jax.random.split(key, 14)
    f32 = jnp.float32
    x_prompt = jax.random.normal(ks[0], (BATCH, SEQ, D_MODEL), f32)
    x_sample = jax.random.normal(ks[1], (DEC_BATCH, DEC_SEQ, D_MODEL), f32)
    norm_g = 1.0 + 0.02 * jax.random.normal(ks[2], (DEPTH, D_MODEL), f32)
    attn_w_in = jax.random.normal(ks[3], (N_ATTN_LAYERS, D_MODEL, ATTN_IN), f32) * D_MODEL ** -0.5
    attn_sink = 0.5 * jax.random.normal(ks[4], (N_ATTN_LAYERS, ATTN_HEADS), f32)
    attn_w_out = jax.random.normal(ks[5], (N_ATTN_LAYERS, ATTN_WIDTH, D_MODEL), f32) * ATTN_WIDTH ** -0.5
    mlstm_w_in = jax.random.normal(ks[6], (N_MLSTM_LAYERS, D_MODEL, MLSTM_IN), f32) * D_MODEL ** -0.5
    ig_bias = 0.1 * jax.random.normal(ks[7], (N_MLSTM_LAYERS, 2, MLSTM_HEADS), f32)
    fg_bias = jnp.linspace(3.0, 6.0, MLSTM_HEADS, dtype=f32)[None, None, :] + 0.1 * jax.random.normal(ks[8], (N_MLSTM_LAYERS, 2, MLSTM_HEADS), f32)
    mlstm_gate_bias = jnp.stack([ig_bias[:, 0], fg_bias[:, 0], ig_bias[:, 1], fg_bias[:, 1]], axis=1).reshape(N_MLSTM_LAYERS, MLSTM_N_GATES)
    mlstm_head_norm = 1.0 + 0.02 * jax.random.normal(ks[9], (N_MLSTM_LAYERS, MLSTM_WIDTH), f32)
    mlstm_w_out = jax.random.normal(ks[10], (N_MLSTM_LAYERS, MLSTM_WIDTH, D_MODEL), f32) * MLSTM_WIDTH ** -0.5
    final_norm_g = 1.0 + 0.02 * jax.random.normal(ks[11], (D_MODEL,), f32)
    return {'x_prompt': x_prompt, 'x_sample': x_sample, 'norm_g': norm_g,
            'attn_w_in': attn_w_in, 'attn_sink': attn_sink, 'attn_w_out': attn_w_out,
            'mlstm_w_in': mlstm_w_in, 'mlstm_gate_bias': mlstm_gate_bias,
            'mlstm_head_norm': mlstm_head_norm, 'mlstm_w_out': mlstm_w_out,
            'final_norm_g': final_norm_g}


def reference(x_prompt, x_sample, norm_g, attn_w_in, attn_sink, attn_w_out, mlstm_w_in, mlstm_gate_bias, mlstm_head_norm, mlstm_w_out, final_norm_g):
    y_prompt = trunk(x_prompt, norm_g, attn_w_in, attn_sink, attn_w_out, mlstm_w_in, mlstm_gate_bias, mlstm_head_norm, mlstm_w_out, final_norm_g)
    y_sample = trunk(x_sample, norm_g, attn_w_in, attn_sink, attn_w_out, mlstm_w_in, mlstm_gate_bias, mlstm_head_norm, mlstm_w_out, final_norm_g)
    return (y_prompt, y_sample)
```

```python
import math
from contextlib import ExitStack

import numpy as np
import ml_dtypes
import concourse.bass as bass
import concourse.mybir as mybir
from concourse.bass_utils import run_bass_kernel_spmd

F32 = mybir.dt.float32
BF16 = mybir.dt.bfloat16
AF = mybir.ActivationFunctionType
ALU = mybir.AluOpType

D = 1024
EPS = 1e-6
ATT_IN = 2560
ML_IN = 4112
NEG = -30000.0


class Prog:
    ENGS = ("pe", "act", "dve", "pool", "sp")

    def __init__(self, nc, stack):
        self.nc = nc
        self.stack = stack
        self.q = {e: [] for e in self.ENGS}
        self.cnt = {e: 0 for e in self.ENGS}
        self.waited = {e: {} for e in self.ENGS}
        self.res = {}
        self.sems = {}
        self.dcnt = {}
        self.alias = {}
        for e in ("pe", "act", "dve", "pool"):
            self.sems[("eng", e)] = stack.enter_context(nc.semaphore("s_" + e))

    def _sem(self, key):
        if key not in self.sems:
            self.sems[key] = self.stack.enter_context(nc_sem(self.nc, "d_%d" % len(self.sems)))
            self.dcnt[key] = 0
        return self.sems[key]

    def op(self, eng, fn, reads=(), writes=(), accum=False, inc=True, dma=None):
        need = {}
        reads = [self.alias.get(r, r) for r in reads]
        writes = [self.alias.get(w, w) for w in writes]
        if dma is not None:
            dma = self.alias.get(dma, dma)

        def add(tok):
            k, v = tok
            if eng == "pe" and k == ("eng", "pe"):
                return
            if need.get(k, 0) < v:
                need[k] = v

        for r in reads:
            st = self.res.get(r)
            if st:
                for tok in st[0].items():
                    add(tok)
        for w in writes:
            st = self.res.get(w)
            if st:
                if accum:
                    for tok in st[2].items():
                        add(tok)
                else:
                    for tok in st[0].items():
                        add(tok)
                for tok in st[1].items():
                    add(tok)
        wl = []
        wd = self.waited[eng]
        for k, v in need.items():
            if wd.get(k, 0) < v:
                wd[k] = v
                wl.append((self.sems[k], v))
        if dma is not None:
            key = ("dma", dma)
            sem = self._sem(key)
            self.dcnt[key] += 16
            tok = (key, self.dcnt[key])
            incinfo = (sem, 16)
        else:
            key = ("eng", eng)
            if inc:
                self.cnt[eng] += 1
                tok = (key, self.cnt[eng])
                incinfo = (self.sems[key], 1)
            else:
                tok = (key, self.cnt[eng] + 1)
                incinfo = None
        self.q[eng].append((wl, fn, incinfo))
        for r in reads:
            st = self.res.setdefault(r, ({}, {}, {}))
            if st[1].get(tok[0], 0) < tok[1]:
                st[1][tok[0]] = tok[1]
        for w in writes:
            st = self.res.setdefault(w, ({}, {}, {}))
            if accum:
                st[0][tok[0]] = max(st[0].get(tok[0], 0), tok[1])
            else:
                st[0].clear()
                st[1].clear()
                st[2].clear()
                st[2].update(need)
                st[0][tok[0]] = tok[1]

    def emit(self, eng, e):
        for wl, fn, incinfo in self.q[eng]:
            for sem, v in wl:
                e.wait_ge(sem, v)
            ins = fn(e)
            if incinfo is not None:
                ins.then_inc(incinfo[0], incinfo[1])

    def final_waits(self, e):
        for key, v in self.dcnt.items():
            e.wait_ge(self.sems[key], v)


def nc_sem(nc, name):
    return nc.semaphore(name)


def build_nc(NBLK, SLOT, depth=4, dbg=0):
    nc = bass.Bass("TRN2", target_bir_lowering=False)
    NTOK = NBLK * 128

    def dram_in(name, shape):
        return nc.dram_tensor(name, list(shape), F32, kind="ExternalInput").ap()

    xin = dram_in("x", [NTOK, D])
    tab = dram_in("tab", [NTOK, 128])
    rbias = dram_in("rbias", [128, 2 * NBLK])
    bmask = nc.dram_tensor("bmask", [128, 1024], BF16, kind="ExternalInput").ap()
    norm_g = dram_in("norm_g", [4, D])
    attn_w_in = dram_in("attn_w_in", [2, D, ATT_IN])
    attn_sink = dram_in("attn_sink", [2, 16])
    attn_w_out = dram_in("attn_w_out", [2, D, D])
    mlstm_w_in = dram_in("mlstm_w_in", [2, D, ML_IN])
    mlstm_gate_bias = dram_in("mlstm_gate_bias", [2, 16])
    mlstm_head_norm = dram_in("mlstm_head_norm", [2, D])
    mlstm_w_out = dram_in("mlstm_w_out", [2, D, D])
    final_norm_g = dram_in("final_norm_g", [1, D])
    y = nc.dram_tensor("y", [NTOK, D], F32, kind="ExternalOutput").ap()
    xA = nc.dram_tensor("xA", [NTOK, D], F32).ap()
    hB = nc.dram_tensor("hB", [NTOK, D], F32).ap()
    s_qk = nc.dram_tensor("s_qk", [NTOK, D], BF16).ap()
    s_kb = nc.dram_tensor("s_kb", [NTOK, 512], BF16).ap()
    s_v = nc.dram_tensor("s_v", [NTOK, D], BF16).ap()
    s_gz = nc.dram_tensor("s_gz", [NTOK, D], BF16).ap()
    s_gt = nc.dram_tensor("s_gt", [NTOK, 16], F32).ap()

    with ExitStack() as stack:
        P = Prog(nc, stack)
        T = {}

        def sb(name, shape, dt=F32):
            T[name] = stack.enter_context(nc.sbuf_tensor(name, list(shape), dt))
            return T[name]

        def ps(name, shape, dt=F32):
            T[name] = stack.enter_context(nc.psum_tensor(name, list(shape), dt))
            return T[name]

        pv = ps("pv", [128, 1024])
        mm = [ps("mm0", [128, 512]), ps("mm1", [128, 512])]
        stp = [ps("st0", [128, 512]), ps("st1", [128, 512])]
        tp = ps("tp", [128, 1024], BF16)
        sm = ps("sm", [128, 512])

        ident = sb("ident", [128, 128], BF16)
        mask_ge = sb("mask_ge", [128, 512], BF16)
        mask_le = sb("mask_le", [128, 512], BF16)
        bm = sb("bm", [128, 1024], BF16)
        Mgt = sb("Mgt", [128, 128], F32)
        Mlt = sb("Mlt", [128, 128], F32)
        ones = sb("ones", [128, 128], F32)
        gb = sb("gb", [128, 2, D], F32)
        hng = sb("hng", [128, D], F32)
        gbias = sb("gbias", [128, 2, 16], F32)
        esink = sb("esink", [128, 2, 16], F32)
        rb = sb("rb", [128, 2 * NBLK], F32)
        w_in = sb("w_in", [128, 8, ML_IN], BF16)
        w_out = sb("w_out", [128, 8, D], BF16)
        NXS = 3
        xs = [sb("xs%d" % i, [128, D]) for i in range(5)]
        tabs = [sb("tabs%d" % i, [128, 128]) for i in range(NXS)]
        junk = sb("junk", [128, D], BF16)
        stt = [sb("stt%d" % i, [128, 8]) for i in range(2)]
        xn = [sb("xn0", [128, D], BF16)] * 2
        xnT = [sb("xnT0", [128, 8, 128], BF16)] * 2
        P.alias.update({"xn1": "xn0", "xnT1": "xnT0", "xs5": "hbt0", "szs2": "vt"})
        qf = sb("qf", [128, D])
        kf = sb("kf", [128, 256])
        rt1 = sb("rt1", [128, D])
        rt2 = sb("rt2", [128, D])
        kt1 = sb("kt1", [128, 256])
        kt2 = sb("kt2", [128, 256])
        qr = sb("qr", [128, D], BF16)
        kr = sb("kr", [128, 4, 2, 128], BF16)
        NQ = 2
        qT = [sb("qT%d" % i, [128, 8, 128], BF16) for i in range(NQ)]
        NK = 3
        kTd = [sb("kTd%d" % i, [128, 8, 128], BF16) for i in range(NK)]
        vaug = [sb("vaug%d" % i, [128, 4, 72], BF16) for i in range(4)]
        szs = [sb("szs%d" % i, [128, D], BF16) for i in range(NQ)]
        NPT = 4
        PT = [sb("PT%d" % i, [128, 512], BF16) for i in range(NPT)]
        dn = sb("dn", [128, 8])
        rdn = sb("rdn", [128, 8])
        o1 = sb("o1", [128, 512])
        og = sb("og", [128, D], BF16)
        ogT = sb("ogT", [128, 8, 128], BF16)
        xo = [sb("xo%d" % i, [128, D]) for i in range(2)]
        wst = xo
        qb = sb("qb", [128, 512], BF16)
        kbs = [sb("kb0", [128, 512], BF16), sb("kb1", [128, 512], BF16)]
        gzb = [szs[0], szs[1]]
        P.alias.update({"gzb0": "szs0", "gzb1": "szs1", "wst0": "xo0", "wst1": "xo1"})
        vf = sb("vf", [128, 4, 260])
        so = qf
        szm = rt1
        P.alias.update({"so": "qf", "szm": "rt1", "h1": "rt2", "hsum": "rt2", "hn": "rt2", "gz": "rt1", "gz2": "qf",
                        "qkT": "qT0", "yo0": "qf", "yo1": "rt1", "hbl0": "hbt0", "hbl1": "hbt1"})
        gt = sb("gt", [128, 16])
        gt3 = [sb("gt3_%d" % i, [128, 16]) for i in range(3)]
        qkT = qT[0]
        gsms = [sb("gsm0", [128, 8, 4]), sb("gsm1", [128, 8, 4])]
        osm = sb("osm", [128, 8, 4])
        vt = sb("vt", [128, 4, 260], BF16)
        Cst = sb("Cst", [128, 4, 260])
        cbf = sb("cbf", [128, 4, 260], BF16)
        hbt = [sb("hbt%d" % i, [128, D]) for i in range(2)]
        hbl = [hbt[0], hbt[1]]
        h1 = rt2
        hsum = rt2
        hn = rt2
        gz = rt1
        gz2 = qf
        yo = [qf, rt1]
        xsa = xs + [hbt[0]]
        szs.append(vt[:].rearrange("p a b -> p (a b)")[:, 0:1024])

        def affine(tile_ap, pattern, cmp, name, cm=1):
            P.op("pool", lambda e: e.memset(tile_ap, 1.0), writes=[name])
            P.op("pool", lambda e: e.affine_select(out=tile_ap, in_=tile_ap, pattern=pattern, compare_op=cmp,
                                                   fill=0.0, base=0, channel_multiplier=cm),
                 reads=[name], writes=[name])

        affine(ident[:], [[-1, 128]], ALU.is_equal, "ident")
        affine(mask_ge[:].rearrange("p (a b) -> p a b", a=4), [[0, 4], [-1, 128]], ALU.is_ge, "mask_ge")
        affine(mask_le[:].rearrange("p (a b) -> p a b", a=4), [[0, 4], [1, 128]], ALU.is_ge, "mask_le", cm=-1)
        affine(Mgt[:], [[-1, 128]], ALU.is_gt, "Mgt")
        affine(Mlt[:], [[1, 128]], ALU.is_gt, "Mlt", cm=-1)
        P.op("pool", lambda e: e.memset(ones[:], 1.0), writes=["ones"])
        for i in range(4):
            P.op("pool", lambda e, i=i: e.memset(vaug[i][:].rearrange("p a b -> p (a b)"), 1.0), writes=["vaug%d" % i])
        P.op("pool", lambda e: e.memset(vf[:], 1.0), writes=["vf"])
        P.op("pool", lambda e: e.memset(kr[:].rearrange("p a b c -> p (a b c)"), 0.0), writes=["kr"])

        P.op("sp", lambda e: e.dma_start(out=bm[:], in_=bmask[:, :]), writes=["bm"], dma="bm")
        P.op("sp", lambda e: e.dma_start(out=rb[:], in_=rbias[:, :]), writes=["rb"], dma="rb")
        P.op("sp", lambda e: e.dma_start(out=gb[:, 1, :], in_=final_norm_g[0, :].partition_broadcast(128)),
             writes=["gbf"], dma="gbf")
        for j in range(2):
            P.op("sp", lambda e, j=j: e.dma_start(out=gbias[:, j, :], in_=mlstm_gate_bias[j, :].partition_broadcast(128)),
                 writes=["gbias"], accum=True, dma="gbias")
            P.op("sp", lambda e, j=j: e.dma_start(out=esink[:, j, :], in_=attn_sink[j, :].partition_broadcast(128)),
                 writes=["esink"], accum=True, dma="esink")
        P.op("act", lambda e: e.activation(out=esink[:], in_=esink[:], func=AF.Exp), reads=["esink"], writes=["esink"])

        wctr = [0]

        def load_weight(dst, src2d, N, name):
            npieces = (N + 1023) // 1024
            first = True
            for c in range(8):
                for pc in range(npieces):
                    n0 = pc * 1024
                    n1 = min(N, n0 + 1024)
                    s = wctr[0] % 2
                    wctr[0] += 1
                    P.op("sp", lambda e, s=s, c=c, n0=n0, n1=n1: e.dma_start(
                        out=wst[s][:, 0:n1 - n0], in_=src2d[c * 128:(c + 1) * 128, n0:n1]),
                        writes=["wst%d" % s], dma="wst%d" % s)
                    eng = ("act", "dve", "pool")[wctr[0] % 3]
                    if eng == "act":
                        f = lambda e, s=s, c=c, n0=n0, n1=n1: e.activation(out=dst[:, c, n0:n1], in_=wst[s][:, 0:n1 - n0], func=AF.Copy)
                    else:
                        f = lambda e, s=s, c=c, n0=n0, n1=n1: e.tensor_copy(out=dst[:, c, n0:n1], in_=wst[s][:, 0:n1 - n0])
                    P.op(eng, f, reads=["wst%d" % s], writes=[name], accum=not first)
                    first = False

        def load_x(src, i, layer_tag, with_tab=False, with_hb=False):
            s = i % NXS
            P.op("sp", lambda e: e.dma_start(out=xs[s][:], in_=src[i * 128:(i + 1) * 128, :]),
                 reads=[(layer_tag, i)], writes=["xs%d" % s], dma="xs%d" % s)
            if with_tab:
                P.op("sp", lambda e: e.dma_start(out=tabs[s][:], in_=tab[i * 128:(i + 1) * 128, :]),
                     writes=["tabs%d" % s], dma="tabs%d" % s)
            if with_hb:
                hs_ = i % 3
                P.op("sp", lambda e: e.dma_start(out=hbl[hs_][:], in_=hB[i * 128:(i + 1) * 128, :]),
                     reads=[("hB", i)], writes=["hbl%d" % hs_], dma="hbl%d" % hs_)

        def rms_stats(src_tile, src_name, st, st_name, extra_scale=None):
            P.op("act", lambda e: e.activation(out=junk[:], in_=src_tile[:], func=AF.Square, accum_out=st[:, 0:1]),
                 reads=[src_name], writes=[st_name + "a", "junk"])
            P.op("act", lambda e: e.activation(out=st[:, 1:2], in_=st[:, 0:1], func=AF.Ln, scale=1.0 / D, bias=EPS),
                 reads=[st_name + "a"], writes=[st_name + "b"])
            P.op("act", lambda e: e.activation(out=st[:, 2:3], in_=st[:, 1:2], func=AF.Exp, scale=-0.5),
                 reads=[st_name + "b"], writes=[st_name])
            if extra_scale is not None:
                P.op("dve", lambda e: e.tensor_scalar_mul(out=st[:, 3:4], in0=st[:, 2:3], scalar1=float(extra_scale)),
                     reads=[st_name], writes=[st_name + "c"])

        def norm_and_transpose(i, layer, extra_scale):
            s = i % NXS
            p = i % 2
            st = stt[p]
            stn = "stt%d" % p
            rms_stats(xs[s], "xs%d" % s, st, stn, extra_scale)
            P.op("dve", lambda e: e.scalar_tensor_tensor(out=xn[p][:], in0=xs[s][:], scalar=st[:, 2:3], in1=gb[:, 0, :],
                                                         op0=ALU.mult, op1=ALU.mult),
                 reads=["xs%d" % s, stn, "gb"], writes=["xn%d" % p])
            for c in range(8):
                P.op("pe", lambda e, c=c: e.transpose(out=tp[:, c * 128:(c + 1) * 128], in_=xn[p][:, c * 128:(c + 1) * 128],
                                                      identity=ident[:]),
                     reads=["xn%d" % p, "ident"], writes=["tp"], accum=(c > 0), inc=(c == 7))
            P.op("dve", lambda e: e.tensor_copy(out=xnT[p][:].rearrange("p c t -> p (c t)"), in_=tp[:]),
                 reads=["tp"], writes=["xnT%d" % p])
            return s, p, st, stn

        mmctr = [0]

        def proj_piece(p, n0, n1):
            b = mmctr[0] % 2
            mmctr[0] += 1
            for c in range(8):
                P.op("pe", lambda e, c=c: e.matmul(mm[b][:, 0:n1 - n0], lhsT=xnT[p][:, c, :], rhs=w_in[:, c, n0:n1],
                                                   start=(c == 0), stop=(c == 7)),
                     reads=["xnT%d" % p, "w_in"], writes=["mm%d" % b], accum=(c > 0), inc=(c == 7))
            return b

        def out_proj_and_store(i, s, srcT, srcT_name, dst, dst_tag, final_norm, xsl=None):
            xsl = xsl or xs
            ob = i % 2
            for half in range(2):
                b = mmctr[0] % 2
                mmctr[0] += 1
                for c in range(8):
                    P.op("pe", lambda e, c=c, b=b, half=half: e.matmul(mm[b][:], lhsT=srcT[:, c, :],
                                                                      rhs=w_out[:, c, half * 512:(half + 1) * 512],
                                                                      start=(c == 0), stop=(c == 7)),
                         reads=[srcT_name, "w_out"], writes=["mm%d" % b], accum=(c > 0), inc=(c == 7))
                P.op("dve", lambda e, b=b, half=half: e.tensor_tensor(out=xo[ob][:, half * 512:(half + 1) * 512],
                                                                      in0=mm[b][:], in1=xsl[s][:, half * 512:(half + 1) * 512],
                                                                      op=ALU.add),
                     reads=["mm%d" % b, "xs%d" % s], writes=["xo%d" % ob], accum=(half > 0))
            if not final_norm:
                P.op("sp", lambda e: e.dma_start(out=dst[i * 128:(i + 1) * 128, :], in_=xo[ob][:]),
                     reads=["xo%d" % ob], writes=[(dst_tag, i)], dma="xo%d" % ob)
            else:
                st = stt[ob]
                stn = "fst%d" % ob
                fstt = fst[ob]
                rms_stats(xo[ob], "xo%d" % ob, fstt, stn, None)
                P.op("dve", lambda e: e.scalar_tensor_tensor(out=yo[ob][:], in0=xo[ob][:], scalar=fstt[:, 2:3], in1=gb[:, 1, :],
                                                             op0=ALU.mult, op1=ALU.mult),
                     reads=["xo%d" % ob, stn, "gbf"], writes=["yo%d" % ob])
                P.op("sp", lambda e: e.dma_start(out=y[i * 128:(i + 1) * 128, :], in_=yo[ob][:]),
                     reads=["yo%d" % ob], writes=[("y", i)], dma="yo%d" % ob)

        fst = [sb("fst%d" % i, [128, 8]) for i in range(2)]

        def attn_layer(layer, j, src, src_tag, dst, dst_tag, final_norm):
            P.op("sp", lambda e: e.dma_start(out=gb[:, 0, :], in_=norm_g[layer, :].partition_broadcast(128)),
                 writes=["gb"], dma="gb")
            load_weight(w_in, attn_w_in[j], ATT_IN, "w_in")
            load_weight(w_out, attn_w_out[j], D, "w_out")
            AX = 6
            NV = 4
            NS = 3
            tpB = sm[:].bitcast(BF16)
            ptc = [0]
            stc = [0]

            def ok(b):
                return 0 <= b < NBLK

            def L(b):
                s = b % AX
                P.op("sp", lambda e: e.dma_start(out=xsa[s][:], in_=src[b * 128:(b + 1) * 128, :]),
                     reads=[(src_tag, b)], writes=["xs%d" % s], dma="xs%d" % s)

            def LT(b):
                s = b % 3
                P.op("sp", lambda e: e.dma_start(out=tabs[s][:], in_=tab[b * 128:(b + 1) * 128, :]),
                     writes=["tabs%d" % s], dma="tabs%d" % s)

            def N0(b):
                s = b % AX
                st = stt[b % 2]
                stn = "stt%d" % (b % 2)
                rms_stats(xsa[s], "xs%d" % s, st, stn, None)
                P.op("dve", lambda e: e.scalar_tensor_tensor(out=xn[0][:], in0=xsa[s][:], scalar=st[:, 2:3], in1=gb[:, 0, :],
                                                             op0=ALU.mult, op1=ALU.mult),
                     reads=["xs%d" % s, stn, "gb"], writes=["xn0"])

            def N1(b):
                for c in range(8):
                    P.op("pe", lambda e, c=c: e.transpose(out=tpB[:, c * 128:(c + 1) * 128], in_=xn[0][:, c * 128:(c + 1) * 128],
                                                          identity=ident[:]),
                         reads=["xn0", "ident"], writes=["sm"], accum=(c > 0), inc=(c == 7))
                P.op("dve", lambda e: e.tensor_copy(out=xnT[0][:].rearrange("p c t -> p (c t)"), in_=tpB),
                     reads=["sm"], writes=["xnT0"])

            def P_pieces(b):
                sq = b % NS
                sv = b % NV
                pcs = []

                def pq(h2):
                    bk = proj_piece(0, h2 * 512, (h2 + 1) * 512)
                    P.op("act", lambda e: e.activation(out=qf[:, h2 * 512:(h2 + 1) * 512], in_=mm[bk][:], func=AF.Copy, scale=0.125),
                         reads=["mm%d" % bk], writes=["qf"], accum=(h2 > 0))

                def pkv():
                    bk = proj_piece(0, 1024, 1536)
                    P.op("act", lambda e: e.activation(out=kf[:], in_=mm[bk][:, 0:256], func=AF.Copy),
                         reads=["mm%d" % bk], writes=["kf"])
                    P.op("act", lambda e: e.activation(out=vaug[sv][:, :, 0:64],
                                                       in_=mm[bk][:, 256:512].rearrange("p (g d) -> p g d", g=4), func=AF.Copy),
                         reads=["mm%d" % bk], writes=["vaug%d" % sv])

                def pz(h2):
                    bk = proj_piece(0, 1536 + h2 * 512, 2048 + h2 * 512)
                    P.op("act", lambda e: e.activation(out=szs[sq][:, h2 * 512:(h2 + 1) * 512], in_=mm[bk][:], func=AF.Silu),
                         reads=["mm%d" % bk], writes=["szs%d" % sq], accum=(h2 > 0))

                def rope():
                    tb = tabs[b % 3]
                    tn = "tabs%d" % (b % 3)
                    q3 = qf[:].rearrange("p (h d) -> p h d", h=16)
                    t13 = rt1[:].rearrange("p (h d) -> p h d", h=16)
                    t23 = rt2[:].rearrange("p (h d) -> p h d", h=16)
                    cosb = tb[:, 0:64].unsqueeze(1).to_broadcast([128, 16, 64])
                    nsin = tb[:, 64:96].unsqueeze(1).to_broadcast([128, 16, 32])
                    psin = tb[:, 96:128].unsqueeze(1).to_broadcast([128, 16, 32])
                    P.op("dve", lambda e: e.tensor_tensor(out=t13, in0=q3, in1=cosb, op=ALU.mult),
                         reads=["qf", tn], writes=["rt1"])
                    P.op("dve", lambda e: e.tensor_tensor(out=t23[:, :, 0:32], in0=q3[:, :, 32:64], in1=nsin, op=ALU.mult),
                         reads=["qf", tn], writes=["rt2"])
                    P.op("dve", lambda e: e.tensor_tensor(out=t23[:, :, 32:64], in0=q3[:, :, 0:32], in1=psin, op=ALU.mult),
                         reads=["qf", tn], writes=["rt2"], accum=True)
                    P.op("pool", lambda e: e.tensor_tensor(out=qr[:], in0=rt1[:], in1=rt2[:], op=ALU.add),
                         reads=["rt1", "rt2"], writes=["qr"])
                    k3 = kf[:].rearrange("p (h d) -> p h d", h=4)
                    kt13 = kt1[:].rearrange("p (h d) -> p h d", h=4)
                    kt23 = kt2[:].rearrange("p (h d) -> p h d", h=4)
                    cosk = tb[:, 0:64].unsqueeze(1).to_broadcast([128, 4, 64])
                    nsink = tb[:, 64:96].unsqueeze(1).to_broadcast([128, 4, 32])
                    psink = tb[:, 96:128].unsqueeze(1).to_broadcast([128, 4, 32])
                    P.op("dve", lambda e: e.tensor_tensor(out=kt13, in0=k3, in1=cosk, op=ALU.mult),
                         reads=["kf", tn], writes=["kt1"])
                    P.op("dve", lambda e: e.tensor_tensor(out=kt23[:, :, 0:32], in0=k3[:, :, 32:64], in1=nsink, op=ALU.mult),
                         reads=["kf", tn], writes=["kt2"])
                    P.op("dve", lambda e: e.tensor_tensor(out=kt23[:, :, 32:64], in0=k3[:, :, 0:32], in1=psink, op=ALU.mult),
                         reads=["kf", tn], writes=["kt2"], accum=True)
                    P.op("dve", lambda e: e.tensor_tensor(out=kr[:, :, 0, 0:64], in0=kt13, in1=kt23, op=ALU.add),
                         reads=["kt1", "kt2"], writes=["kr"])
                    P.op("dve", lambda e: e.tensor_tensor(out=kr[:, :, 1, 64:128], in0=kt13, in1=kt23, op=ALU.add),
                         reads=["kt1", "kt2"], writes=["kr"], accum=True)

                return [lambda: pq(0), lambda: pq(1), pkv, lambda: pz(0), rope, lambda: pz(1)]

            def T(b):
                sq = b % 2
                sk = b % NK
                for c in range(8):
                    P.op("pe", lambda e, c=c: e.transpose(out=tpB[:, c * 128:(c + 1) * 128], in_=qr[:, c * 128:(c + 1) * 128],
                                                          identity=ident[:]),
                         reads=["qr", "ident"], writes=["sm"], accum=(c > 0), inc=(c == 7))
                P.op("dve", lambda e: e.tensor_copy(out=qT[sq][:].rearrange("p c t -> p (c t)"), in_=tpB),
                     reads=["sm"], writes=["qT%d" % sq])
                for g in range(8):
                    P.op("pe", lambda e, g=g: e.transpose(out=tp[:, g * 128:(g + 1) * 128], in_=kr[:, g // 2, g % 2, :], identity=ident[:]),
                         reads=["kr", "ident"], writes=["tp"], accum=(g > 0), inc=(g == 7))
                P.op("act", lambda e: e.activation(out=kTd[sk][:].rearrange("p c t -> p (c t)"), in_=tp[:], func=AF.Copy),
                     reads=["tp"], writes=["kTd%d" % sk])

            def S(i, pieces):
                sq = i % 2
                ss = i % NS
                nbrs = []
                if i > 0:
                    nbrs.append((i - 1, bm[:, 0:512] if (i % SLOT == 0) else mask_ge[:], "bm" if (i % SLOT == 0) else "mask_ge"))
                nbrs.append((i, None, None))
                if i < NBLK - 1:
                    nbrs.append((i + 1, bm[:, 512:1024] if ((i + 1) % SLOT == 0) else mask_le[:],
                                 "bm" if ((i + 1) % SLOT == 0) else "mask_le"))
                for half in range(2):
                    for gl in range(2):
                        g = half * 2 + gl
                        pts = []
                        for jn, (jb, mk, mkn) in enumerate(nbrs):
                            skj = jb % NK
                            svj = jb % NV
                            sb_ = stc[0] % 2
                            stc[0] += 1
                            for par in range(2):
                                P.op("pe", lambda e, par=par, sb_=sb_, skj=skj, g=g: e.matmul(
                                    stp[sb_][:, par * 256:(par + 1) * 256],
                                    lhsT=kTd[skj][:, g * 2 + par, :],
                                    rhs=qT[sq][:, 2 * g:2 * g + 2, :].rearrange("p c t -> p (c t)"),
                                    start=True, stop=True),
                                    reads=["kTd%d" % skj, "qT%d" % sq], writes=["st%d" % sb_], accum=(par > 0), inc=(par == 1))
                            pt = ptc[0] % NPT
                            ptc[0] += 1
                            P.op("act", lambda e, pt=pt, sb_=sb_: e.activation(out=PT[pt][:], in_=stp[sb_][:], func=AF.Exp),
                                 reads=["st%d" % sb_], writes=["PT%d" % pt])
                            if mk is not None:
                                P.op("dve", lambda e, pt=pt, mk=mk: e.tensor_tensor(out=PT[pt][:], in0=PT[pt][:], in1=mk, op=ALU.mult),
                                     reads=["PT%d" % pt, mkn], writes=["PT%d" % pt])
                            pts.append((pt, svj))
                        if pieces:
                            pieces.pop(0)()
                        for hh in range(4):
                            hl = gl * 4 + hh
                            for jn, (pt, svj) in enumerate(pts):
                                P.op("pe", lambda e, hh=hh, hl=hl, pt=pt, svj=svj, jn=jn, g=g, npts=len(pts): e.matmul(
                                    pv[:, hl * 128:hl * 128 + 66], lhsT=PT[pt][:, (0, 2, 1, 3)[hh] * 128:((0, 2, 1, 3)[hh] + 1) * 128],
                                    rhs=vaug[svj][:, g, 0:66], start=(jn == 0), stop=(jn == npts - 1)),
                                    reads=["PT%d" % pt, "vaug%d" % svj], writes=["pv"],
                                    accum=not (jn == 0 and hh == 0 and gl == 0), inc=(hh == 3 and jn == len(pts) - 1))
                    pv3 = pv[:].rearrange("p (h d) -> p h d", h=8)
                    P.op("dve", lambda e, half=half: e.tensor_tensor(out=dn[:], in0=pv3[:, :, 64], in1=esink[:, j, half * 8:(half + 1) * 8],
                                                                     op=ALU.add),
                         reads=["pv", "esink"], writes=["dn"])
                    P.op("dve", lambda e: e.reciprocal(out=rdn[:], in_=dn[:]), reads=["dn"], writes=["rdn"])
                    P.op("dve", lambda e: e.tensor_tensor(out=o1[:].rearrange("p (h d) -> p h d", h=8), in0=pv3[:, :, 0:64],
                                                          in1=rdn[:].unsqueeze(2).to_broadcast([128, 8, 64]), op=ALU.mult),
                         reads=["pv", "rdn"], writes=["o1"])
                    P.op("pool", lambda e, half=half: e.tensor_tensor(out=og[:, half * 512:(half + 1) * 512], in0=o1[:],
                                                                     in1=szs[ss][:, half * 512:(half + 1) * 512], op=ALU.mult),
                         reads=["o1", "szs%d" % ss], writes=["og"], accum=(half > 0))

            def O1(b):
                for c in range(8):
                    P.op("pe", lambda e, c=c: e.transpose(out=tp[:, c * 128:(c + 1) * 128], in_=og[:, c * 128:(c + 1) * 128],
                                                          identity=ident[:]),
                         reads=["og", "ident"], writes=["tp"], accum=(c > 0), inc=(c == 7))
                P.op("dve", lambda e: e.tensor_copy(out=ogT[:].rearrange("p c t -> p (c t)"), in_=tp[:]),
                     reads=["tp"], writes=["ogT"])

            def O2(b):
                out_proj_and_store(b, b % AX, ogT, "ogT", dst, dst_tag, final_norm, xsl=xsa)

            carry = []
            for i in range(-4, NBLK + 1):
                if ok(i + 4):
                    L(i + 4)
                if ok(i + 3):
                    LT(i + 3)
                while carry:
                    carry.pop(0)()
                if ok(i - 1):
                    O1(i - 1)
                if ok(i + 1):
                    T(i + 1)
                if ok(i + 2):
                    N1(i + 2)
                if ok(i - 1):
                    O2(i - 1)
                pieces = P_pieces(i + 2) if ok(i + 2) else []
                if ok(i):
                    S(i, pieces)
                while len(pieces) > 1:
                    pieces.pop(0)()
                carry = pieces
                if ok(i + 3):
                    N0(i + 3)

        def mlstm_layer(layer, j, src, src_tag, dst, dst_tag, final_norm):
            P.op("sp", lambda e: e.dma_start(out=gb[:, 0, :], in_=norm_g[layer, :].partition_broadcast(128)),
                 writes=["gb"], dma="gb")
            P.op("sp", lambda e: e.dma_start(out=hng[:], in_=mlstm_head_norm[j, :].partition_broadcast(128)),
                 writes=["hng"], dma="hng")
            load_weight(w_in, mlstm_w_in[j], ML_IN, "w_in")
            load_weight(w_out, mlstm_w_out[j], D, "w_out")
            stc = [0]
            MX = 5
            qk3 = [(qT[0], "qT0"), (qT[1], "qT1"), (kTd[0], "kTd0")]
            kb3 = [(kbs[0], "kb0"), (kbs[1], "kb1"), (PT[1], "PT1")]
            gz3 = [(szs[0], "szs0"), (szs[1], "szs1"), (qr, "qr")]
            v3 = [(xn[0][:], "xn0"), (xnT[0][:].rearrange("p c t -> p (c t)"), "xnT0"),
                  (kTd[1][:].rearrange("p c t -> p (c t)"), "kTd1")]

            def run_pass(mode):
                A = (mode == "A")
                dirf = A
                order = list(range(NBLK)) if dirf else list(range(NBLK - 1, -1, -1))
                ig0, fg0 = (0, 4) if dirf else (8, 12)
                Mx, Mn = (Mgt, "Mgt") if dirf else (Mlt, "Mlt")
                mk, mkn = (mask_le, "mask_le") if dirf else (mask_ge, "mask_ge")

                def ok(n):
                    return 0 <= n < NBLK

                def QK(n):
                    return (qkT, "qkT") if A else qk3[n % 3]

                def KB(n):
                    return (kbs[n % 2], "kb%d" % (n % 2)) if A else kb3[n % 3]

                def GT(n):
                    return (gt, "gt") if A else (gt3[n % 3], "gt3_%d" % (n % 3))

                def GZ(n):
                    return (gzb[n % 2], "gzb%d" % (n % 2)) if A else gz3[n % 3]

                def L(n):
                    i = order[n]
                    s = n % MX
                    P.op("sp", lambda e: e.dma_start(out=xs[s][:], in_=src[i * 128:(i + 1) * 128, :]),
                         reads=[(src_tag, i)], writes=["xs%d" % s], dma="xs%d" % s)

                def LB(n):
                    i = order[n]
                    r0, r1 = i * 128, (i + 1) * 128
                    L(n)
                    t, tn = qk3[n % 3]
                    P.op("sp", lambda e: e.dma_start(out=t[:].rearrange("p c t -> p (c t)"), in_=s_qk[r0:r1, :]),
                         reads=[("s_qk", i)], writes=[tn], dma=tn)
                    t2, tn2 = kb3[n % 3]
                    P.op("sp", lambda e: e.dma_start(out=t2[:], in_=s_kb[r0:r1, :]),
                         reads=[("s_kb", i)], writes=[tn2], dma=tn2)
                    t3, tn3 = gz3[n % 3]
                    P.op("sp", lambda e: e.dma_start(out=t3[:], in_=s_gz[r0:r1, :]),
                         reads=[("s_gz", i)], writes=[tn3], dma=tn3)
                    t4, tn4 = v3[n % 3]
                    P.op("sp", lambda e: e.dma_start(out=t4, in_=s_v[r0:r1, :]),
                         reads=[("s_v", i)], writes=[tn4], dma=tn4)
                    t5, tn5 = GT(n)
                    P.op("sp", lambda e: e.dma_start(out=t5[:], in_=s_gt[r0:r1, :]),
                         reads=[("s_gt", i)], writes=[tn5], dma=tn5)

                def LHB(n):
                    i = order[n]
                    hs_ = n % 2
                    P.op("sp", lambda e: e.dma_start(out=hbl[hs_][:], in_=hB[i * 128:(i + 1) * 128, :]),
                         reads=[("hB", i)], writes=["hbl%d" % hs_], dma="hbl%d" % hs_)

                def F0(n):
                    s = n % MX
                    p = n % 2
                    st = stt[p]
                    stn = "stt%d" % p
                    rms_stats(xs[s], "xs%d" % s, st, stn, None)
                    P.op("dve", lambda e: e.scalar_tensor_tensor(out=xn[0][:], in0=xs[s][:], scalar=st[:, 2:3], in1=gb[:, 0, :],
                                                                 op0=ALU.mult, op1=ALU.mult),
                         reads=["xs%d" % s, stn, "gb"], writes=["xn0"])

                def F1(n):
                    for c in range(8):
                        P.op("pe", lambda e, c=c: e.transpose(out=tp[:, c * 128:(c + 1) * 128], in_=xn[0][:, c * 128:(c + 1) * 128],
                                                              identity=ident[:]),
                             reads=["xn0", "ident"], writes=["tp"], accum=(c > 0), inc=(c == 7))
                    P.op("dve", lambda e: e.tensor_copy(out=xnT[0][:].rearrange("p c t -> p (c t)"), in_=tp[:]),
                         reads=["tp"], writes=["xnT0"])

                def F2(n):
                    i = order[n]
                    r0, r1 = i * 128, (i + 1) * 128
                    p = n % 2
                    kbt, kbn = KB(n)
                    b = proj_piece(0, 0, 512)
                    P.op("act", lambda e, b=b: e.activation(out=qb[:], in_=mm[b][:], func=AF.Copy),
                         reads=["mm%d" % b], writes=["qb"])
                    b = proj_piece(0, 512, 1024)
                    P.op("act", lambda e, b=b: e.activation(out=kbt[:], in_=mm[b][:], func=AF.Copy, scale=128.0 ** -0.5),
                         reads=["mm%d" % b], writes=[kbn])
                    P.op("sp", lambda e: e.dma_start(out=s_kb[r0:r1, :], in_=kbt[:]),
                         reads=[kbn], writes=[("s_kb", i)], dma=kbn)
                    for h2 in range(2):
                        b = proj_piece(0, 1024 + h2 * 512, 1536 + h2 * 512)
                        P.op("act", lambda e, b=b, h2=h2: e.activation(out=vf[:, 2 * h2:2 * h2 + 2, 0:256],
                                                                       in_=mm[b][:].rearrange("p (h d) -> p h d", h=2),
                                                                       func=AF.Copy),
                             reads=["mm%d" % b], writes=["vf"], accum=(h2 > 0))
                        P.op("act", lambda e, b=b, h2=h2: e.activation(out=qr[:, h2 * 512:(h2 + 1) * 512], in_=mm[b][:], func=AF.Copy),
                             reads=["mm%d" % b], writes=["qr"], accum=(h2 > 0))
                    P.op("sp", lambda e: e.dma_start(out=s_v[r0:r1, :], in_=qr[:]),
                         reads=["qr"], writes=[("s_v", i)], dma="qr")
                    b = proj_piece(0, 4096, 4112)
                    P.op("dve", lambda e, b=b: e.tensor_tensor(out=gt[:], in0=mm[b][:, 0:16], in1=gbias[:, j, :], op=ALU.add),
                         reads=["mm%d" % b, "gbias"], writes=["gt"])
                    P.op("sp", lambda e: e.dma_start(out=s_gt[r0:r1, :], in_=gt[:]),
                         reads=["gt"], writes=[("s_gt", i)], dma="gt")
                    gsm = gsms[p]
                    gs = "_%d" % p
                    P.op("act", lambda e: e.activation(out=gsm[:, 0, :], in_=gt[:, fg0:fg0 + 4], func=AF.Exp, scale=-1.0),
                         reads=["gt"], writes=["g_e1" + gs])
                    P.op("act", lambda e: e.activation(out=gsm[:, 1, :], in_=gsm[:, 0, :], func=AF.Ln, bias=1.0),
                         reads=["g_e1" + gs], writes=["g_l" + gs])
                    for h2 in range(2):
                        b = proj_piece(0, 2048 + h2 * 512, 2560 + h2 * 512)
                        P.op("act", lambda e, b=b, h2=h2: e.activation(out=so[:, h2 * 512:(h2 + 1) * 512], in_=mm[b][:],
                                                                       func=AF.Sigmoid),
                             reads=["mm%d" % b], writes=["so"], accum=(h2 > 0))
                    for h2 in range(2):
                        b = proj_piece(0, 3072 + h2 * 512, 3584 + h2 * 512)
                        P.op("act", lambda e, b=b, h2=h2: e.activation(out=szm[:, h2 * 512:(h2 + 1) * 512], in_=mm[b][:],
                                                                       func=AF.Silu),
                             reads=["mm%d" % b], writes=["szm"], accum=(h2 > 0))
                    P.op("pool", lambda e: e.tensor_tensor(out=szm[:], in0=so[:], in1=szm[:], op=ALU.mult),
                         reads=["so", "szm"], writes=["szm"])
                    P.op("pool", lambda e: e.tensor_tensor(out=gzb[p][:], in0=szm[:], in1=hng[:], op=ALU.mult),
                         reads=["szm", "hng"], writes=["gzb%d" % p])
                    P.op("sp", lambda e: e.dma_start(out=s_gz[r0:r1, :], in_=gzb[p][:]),
                         reads=["gzb%d" % p], writes=[("s_gz", i)], dma="gzb%d" % p)

                def F3a(n):
                    i = order[n]
                    r0, r1 = i * 128, (i + 1) * 128
                    p = n % 2
                    gsm = gsms[p]
                    gs = "_%d" % p
                    rbcol = i if dirf else NBLK + i
                    gtt, gtn = GT(n)
                    if A:
                        kbt, kbn = KB(n)
                        for c in range(8):
                            srct = qb if c < 4 else kbt
                            P.op("pe", lambda e, c=c, srct=srct: e.transpose(out=tp[:, c * 128:(c + 1) * 128],
                                                                              in_=srct[:, (c % 4) * 128:(c % 4 + 1) * 128], identity=ident[:]),
                                 reads=["qb", kbn, "ident"], writes=["tp"], accum=(c > 0), inc=(c == 7))
                        P.op("dve", lambda e: e.tensor_copy(out=qkT[:].rearrange("p c t -> p (c t)"), in_=tp[:]),
                             reads=["tp"], writes=["qkT"])
                        P.op("sp", lambda e: e.dma_start(out=s_qk[r0:r1, :], in_=qkT[:].rearrange("p c t -> p (c t)")),
                             reads=["qkT"], writes=[("s_qk", i)], dma="qkT")
                    else:
                        P.op("act", lambda e: e.activation(out=gsm[:, 0, :], in_=gtt[:, fg0:fg0 + 4], func=AF.Exp, scale=-1.0),
                             reads=[gtn], writes=["g_e1" + gs])
                        P.op("act", lambda e: e.activation(out=gsm[:, 1, :], in_=gsm[:, 0, :], func=AF.Ln, bias=1.0),
                             reads=["g_e1" + gs], writes=["g_l" + gs])
                    P.op("pe", lambda e: e.matmul(sm[:, 0:4], lhsT=Mx[:], rhs=gsm[:, 1, :], start=True, stop=True),
                         reads=[Mn, "g_l" + gs], writes=["sm"])
                    P.op("pe", lambda e: e.matmul(sm[:, 4:8], lhsT=ones[:], rhs=gsm[:, 1, :], start=True, stop=True),
                         reads=["ones", "g_l" + gs], writes=["sm"], accum=True)
                    P.op("dve", lambda e: e.tensor_tensor(out=gsm[:, 2, :], in0=gtt[:, ig0:ig0 + 4], in1=sm[:, 0:4], op=ALU.subtract),
                         reads=[gtn, "sm"], writes=["g_ta" + gs])
                    P.op("act", lambda e: e.activation(out=gsm[:, 3, :], in_=gsm[:, 2, :], func=AF.Exp),
                         reads=["g_ta" + gs], writes=["g_ea" + gs])
                    P.op("act", lambda e: e.activation(out=gsm[:, 4, :], in_=sm[:, 0:4], func=AF.Exp),
                         reads=["sm"], writes=["g_eo" + gs])
                    P.op("act", lambda e: e.activation(out=gsm[:, 5, :], in_=sm[:, 4:8], func=AF.Exp, scale=-1.0,
                                                       bias=rb[:, rbcol:rbcol + 1]),
                         reads=["sm", "rb"], writes=["g_dec" + gs])

                def F3b(n):
                    p = n % 2
                    gsm = gsms[p]
                    gs = "_%d" % p
                    if A:
                        P.op("dve", lambda e: e.tensor_tensor(out=vt[:], in0=vf[:],
                                                              in1=gsm[:, 3, :].unsqueeze(2).to_broadcast([128, 4, 260]), op=ALU.mult),
                             reads=["vf", "g_ea" + gs], writes=["vt"])
                    else:
                        vb, vbn = v3[n % 3]
                        P.op("dve", lambda e: e.tensor_tensor(out=vt[:, :, 0:256], in0=vb.rearrange("p (h d) -> p h d", h=4),
                                                              in1=gsm[:, 3, :].unsqueeze(2).to_broadcast([128, 4, 256]), op=ALU.mult),
                             reads=[vbn, "g_ea" + gs], writes=["vt"])
                        P.op("dve", lambda e: e.tensor_copy(out=vt[:, :, 256:260],
                                                            in_=gsm[:, 3, :].unsqueeze(2).to_broadcast([128, 4, 4])),
                             reads=["g_ea" + gs], writes=["vt"], accum=True)

                def F4(n):
                    qk, qkn = QK(n)
                    sb_ = stc[0] % 2
                    stc[0] += 1
                    for h in range(4):
                        P.op("pe", lambda e, h=h: e.matmul(stp[sb_][:, h * 128:(h + 1) * 128], lhsT=qk[:, 4 + h, :], rhs=qk[:, h, :],
                                                           start=True, stop=True),
                             reads=[qkn], writes=["st%d" % sb_], accum=(h > 0), inc=(h == 3))
                    P.op("dve", lambda e: e.tensor_tensor(out=PT[0][:], in0=stp[sb_][:], in1=mk[:], op=ALU.mult),
                         reads=["st%d" % sb_, mkn], writes=["PT0"])

                def B1a(n):
                    gsm = gsms[n % 2]
                    gs = "_%d" % (n % 2)
                    for h in range(4):
                        P.op("act", lambda e, h=h: e.activation(out=cbf[:, h, :], in_=Cst[:, h, :], func=AF.Copy, scale=gsm[:, 5, h:h + 1]),
                             reads=["Cst", "g_dec" + gs], writes=["cbf"], accum=(h > 0))

                def B1b(n):
                    qk, qkn = QK(n)
                    for h in range(4):
                        P.op("pe", lambda e, h=h: e.matmul(pv[:, h * 256:(h + 1) * 256], lhsT=PT[0][:, h * 128:(h + 1) * 128],
                                                           rhs=vt[:, h, 0:256], start=True, stop=False),
                             reads=["PT0", "vt"], writes=["pv"], accum=(h > 0), inc=False)
                        P.op("pe", lambda e, h=h: e.matmul(pv[:, h * 256:(h + 1) * 256], lhsT=qk[:, h, :],
                                                           rhs=cbf[:, h, 0:256], start=False, stop=True),
                             reads=[qkn, "cbf"], writes=["pv"], accum=True, inc=False)
                        P.op("pe", lambda e, h=h: e.matmul(sm[:, 16 + 2 * h:18 + 2 * h], lhsT=PT[0][:, h * 128:(h + 1) * 128],
                                                           rhs=vt[:, h, 256:258], start=True, stop=False),
                             reads=["PT0", "vt"], writes=["sm"], accum=True, inc=False)
                        P.op("pe", lambda e, h=h: e.matmul(sm[:, 16 + 2 * h:18 + 2 * h], lhsT=qk[:, h, :],
                                                           rhs=cbf[:, h, 256:258], start=False, stop=True),
                             reads=[qkn, "cbf"], writes=["sm"], accum=True, inc=(h == 3))

                def B2(n):
                    i = order[n]
                    p = n % 2
                    gsm = gsms[p]
                    gs = "_%d" % p
                    kbt, kbn = KB(n)
                    for h in range(4):
                        sb2 = stc[0] % 2
                        stc[0] += 1
                        P.op("pe", lambda e, h=h, sb2=sb2: e.matmul(stp[sb2][:, 0:258], lhsT=kbt[:, h * 128:(h + 1) * 128], rhs=vt[:, h, 0:258],
                                                                    start=True, stop=True),
                             reads=[kbn, "vt"], writes=["st%d" % sb2])
                        P.op("dve", lambda e, h=h, sb2=sb2: e.scalar_tensor_tensor(out=Cst[:, h, 0:258], in0=Cst[:, h, 0:258],
                                                                                   scalar=gsm[:, 5, h:h + 1],
                                                                                   in1=stp[sb2][:, 0:258], op0=ALU.mult, op1=ALU.add),
                             reads=["Cst", "g_dec" + gs, "st%d" % sb2], writes=["Cst"], accum=(h > 0))
                    P.op("dve", lambda e: e.tensor_tensor(out=osm[:, 0, :], in0=sm[:, 16:24].rearrange("p (h t) -> p h t", t=2)[:, :, 0],
                                                          in1=gsm[:, 4, :], op=ALU.mult),
                         reads=["sm", "g_eo" + gs], writes=["o_a1"])
                    P.op("dve", lambda e: e.scalar_tensor_tensor(out=osm[:, 1, :], in0=osm[:, 0, :], scalar=-1.0, in1=osm[:, 0, :],
                                                                 op0=ALU.mult, op1=ALU.max),
                         reads=["o_a1"], writes=["o_a2"])
                    P.op("dve", lambda e: e.tensor_scalar_max(out=osm[:, 2, :], in0=osm[:, 1, :], scalar1=1.0),
                         reads=["o_a2"], writes=["o_a3"])
                    P.op("dve", lambda e: e.reciprocal(out=osm[:, 3, :], in_=osm[:, 2, :]), reads=["o_a3"], writes=["o_ra"])
                    P.op("dve", lambda e: e.tensor_tensor(out=osm[:, 4, :], in0=osm[:, 3, :], in1=gsm[:, 4, :], op=ALU.mult),
                         reads=["o_ra", "g_eo" + gs], writes=["o_hs"])
                    pv3 = pv[:].rearrange("p (h d) -> p h d", h=4)
                    hsb = osm[:, 4, :].unsqueeze(2).to_broadcast([128, 4, 256])
                    if A:
                        hb_ = n % 2
                        P.op("dve", lambda e: e.tensor_tensor(out=hbt[hb_][:].rearrange("p (h d) -> p h d", h=4), in0=pv3, in1=hsb, op=ALU.mult),
                             reads=["pv", "o_hs"], writes=["hbt%d" % hb_])
                        P.op("sp", lambda e: e.dma_start(out=hB[i * 128:(i + 1) * 128, :], in_=hbt[hb_][:]),
                             reads=["hbt%d" % hb_], writes=[("hB", i)], dma="hbt%d" % hb_)
                    else:
                        hs_ = n % 2
                        for h in range(4):
                            P.op("dve", lambda e, h=h: e.scalar_tensor_tensor(out=hsum[:, h * 256:(h + 1) * 256], in0=pv[:, h * 256:(h + 1) * 256],
                                                                              scalar=osm[:, 4, h:h + 1], in1=hbl[hs_][:, h * 256:(h + 1) * 256],
                                                                              op0=ALU.mult, op1=ALU.add),
                                 reads=["pv", "o_hs", "hbl%d" % hs_], writes=["hsum"], accum=(h > 0))

                def B3v1(n):
                    hs_ = n % 2
                    for h in range(4):
                        P.op("act", lambda e, h=h: e.activation(out=junk[:, 0:256], in_=hsum[:, h * 256:(h + 1) * 256], func=AF.Square,
                                                                accum_out=osm[:, 5, h:h + 1]),
                             reads=["hsum"], writes=["o_hss", "junk"])
                    P.op("act", lambda e: e.activation(out=osm[:, 6, :], in_=osm[:, 5, :], func=AF.Ln, scale=1.0 / 256, bias=EPS),
                         reads=["o_hss"], writes=["o_hln"])
                    P.op("act", lambda e: e.activation(out=osm[:, 7, :], in_=osm[:, 6, :], func=AF.Exp, scale=-0.5),
                         reads=["o_hln"], writes=["o_hr"])

                def B3v2(n):
                    gzt, gzn = GZ(n)
                    P.op("dve", lambda e: e.tensor_tensor(out=hn[:].rearrange("p (h d) -> p h d", h=4),
                                                          in0=hsum[:].rearrange("p (h d) -> p h d", h=4),
                                                          in1=osm[:, 7, :].unsqueeze(2).to_broadcast([128, 4, 256]), op=ALU.mult),
                         reads=["hsum", "o_hr"], writes=["hn"])
                    P.op("pool", lambda e: e.tensor_tensor(out=og[:], in0=hn[:], in1=gzt[:], op=ALU.mult),
                         reads=["hn", gzn], writes=["og"])

                def B3pe(n):
                    bk = mmctr[0] % 2
                    mmctr[0] += 1
                    mmb = mm[bk][:].bitcast(BF16)
                    for c in range(8):
                        P.op("pe", lambda e, c=c: e.transpose(out=mmb[:, c * 128:(c + 1) * 128], in_=og[:, c * 128:(c + 1) * 128],
                                                              identity=ident[:]),
                             reads=["og", "ident"], writes=["mm%d" % bk], accum=(c > 0), inc=(c == 7))
                    P.op("dve", lambda e: e.tensor_copy(out=ogT[:].rearrange("p c t -> p (c t)"), in_=mmb),
                         reads=["mm%d" % bk], writes=["ogT"])

                def B4(n):
                    out_proj_and_store(order[n], n % MX, ogT, "ogT", dst, dst_tag, final_norm)

                P.op("pool", lambda e: e.memset(Cst[:].rearrange("p a b -> p (a b)"), 0.0), writes=["Cst"])
                if A:
                    for n in range(min(3, NBLK)):
                        L(n)
                    for n in range(-2, NBLK + 1):
                        if n + 3 >= 3 and ok(n + 3):
                            L(n + 3)
                        if ok(n + 1):
                            F1(n + 1)
                        if ok(n + 2):
                            F0(n + 2)
                        if ok(n + 1):
                            F2(n + 1)
                        if ok(n):
                            B1b(n)
                        if ok(n + 1):
                            F3a(n + 1)
                        if ok(n):
                            B2(n)
                        if ok(n + 1):
                            F4(n + 1)
                            B1a(n + 1)
                            F3b(n + 1)
                else:
                    for n in range(min(2, NBLK)):
                        LB(n)
                    LHB(0)
                    for n in range(-1, NBLK + 1):
                        if n + 2 >= 2 and ok(n + 2):
                            LB(n + 2)
                        if n + 1 >= 1 and ok(n + 1):
                            LHB(n + 1)
                        if ok(n):
                            B1b(n)
                            B2(n)
                        if ok(n + 1):
                            F3a(n + 1)
                        if ok(n - 1):
                            B3pe(n - 1)
                        if ok(n + 1):
                            F3b(n + 1)
                            F4(n + 1)
                            B1a(n + 1)
                        if ok(n):
                            B3v1(n)
                        if ok(n - 1):
                            B4(n - 1)
                        if ok(n):
                            B3v2(n)

            run_pass("A")
            run_pass("B")

        if dbg == 1:
            depth = 0
            P.op("sp", lambda e: e.dma_start(out=y[0:128, :], in_=gb[:, 1, :]), reads=["gbf"], writes=[("y", 0)], dma="dbg")
        if dbg == 2:
            depth = 0
            load_weight(w_in, attn_w_in[0], ATT_IN, "w_in")
            load_x(xin, 0, "xin", with_tab=True)
            P.op("sp", lambda e: e.dma_start(out=gb[:, 0, :], in_=norm_g[0, :].partition_broadcast(128)), writes=["gb"], dma="gb")
            norm_and_transpose(0, 0, None)
            b = proj_piece(0, 0, 512)
            P.op("act", lambda e: e.activation(out=xo[0][:, 0:512], in_=mm[b][:], func=AF.Copy), reads=["mm%d" % b], writes=["xo0"])
            P.op("sp", lambda e: e.dma_start(out=y[0:128, 0:512], in_=xo[0][:, 0:512]), reads=["xo0"], writes=[("y", 0)], dma="dbg")
        if dbg >= 3:
            depth = 1
        for layer in range(depth):
            src, src_tag = (xin, "xin") if layer == 0 else (xA, "xA")
            last = (layer == depth - 1)
            dst, dst_tag = (y, "y") if last else (xA, "xA")
            if layer % 2 == 0:
                attn_layer(layer, layer // 2, src, src_tag, dst, dst_tag, last)
            else:
                mlstm_layer(layer, layer // 2, src, src_tag, dst, dst_tag, last)

        with nc.Block() as block:
            @block.sync
            def _(e):
                P.emit("sp", e)
                P.final_waits(e)

            @block.scalar
            def _(e):
                P.emit("act", e)

            @block.vector
            def _(e):
                P.emit("dve", e)

            @block.gpsimd
            def _(e):
                P.emit("pool", e)

            @block.tensor
            def _(e):
                P.emit("pe", e)
        ninstr = {k: len(v) for k, v in P.q.items()}
    return nc, ninstr


ROPE_THETA = 10000.0


def _rope_tab(pos):
    half = 32
    inv = np.exp(np.float32(-math.log(ROPE_THETA)) * np.arange(half, dtype=np.float32) / np.float32(half)).astype(np.float32)
    ang = (pos.astype(np.float32)[:, None] * inv[None, :]).astype(np.float32)
    c = np.cos(ang.astype(np.float64)).astype(np.float32)
    s = np.sin(ang.astype(np.float64)).astype(np.float32)
    return np.concatenate([c, c, -s, s], axis=1).astype(np.float32)


def _core_inputs(seqs, NBLK, SLOT):
    NTOK = NBLK * 128
    x = np.zeros((NTOK, D), np.float32)
    whole = (len(seqs) == 1 and seqs[0].shape[0] == NTOK)
    if whole:
        x[:] = seqs[0]
        pos = np.arange(NTOK)
    else:
        L = SLOT * 128
        for n, sq in enumerate(seqs):
            assert sq.shape[0] == L
            x[n * L:(n + 1) * L] = sq
        pos = np.arange(NTOK) % L
    tab = _rope_tab(pos)
    rb = np.zeros((128, 2 * NBLK), np.float32)
    jj = np.arange(128)
    m_ge = (jj[:, None] >= jj[None, :]).astype(np.float32)
    m_le = (jj[:, None] <= jj[None, :]).astype(np.float32)
    bmask = np.zeros((128, 1024), np.float32)
    if whole:
        bmask[:, 0:512] = np.tile(m_ge, (1, 4))
        bmask[:, 512:1024] = np.tile(m_le, (1, 4))
    else:
        for i in range(NBLK):
            if i % SLOT == 0:
                rb[:, i] = NEG
            if (i + 1) % SLOT == 0:
                rb[:, NBLK + i] = NEG
    return {"x": x, "tab": tab, "rbias": rb, "bmask": bmask.astype(ml_dtypes.bfloat16)}


_CACHE = {}


def run_cores(core_seqs, weights, NBLK, SLOT, depth=4, dbg=0):
    key = (NBLK, SLOT, depth)
    if key not in _CACHE:
        _CACHE[key] = build_nc(NBLK, SLOT, depth, dbg)
    nc, _ = _CACHE[key]
    w = {k: np.ascontiguousarray(np.asarray(v, dtype=np.float32)) for k, v in weights.items()}
    w["final_norm_g"] = w["final_norm_g"].reshape(1, D)
    in_maps = []
    for seqs in core_seqs:
        m = _core_inputs(seqs, NBLK, SLOT)
        m.update(w)
        in_maps.append(m)
    res = run_bass_kernel_spmd(nc, in_maps, core_ids=list(range(len(core_seqs))))
    return [r["y"] for r in res.results]


def kernel(x_prompt, x_sample, norm_g, attn_w_in, attn_sink, attn_w_out, mlstm_w_in, mlstm_gate_bias,
           mlstm_head_norm, mlstm_w_out, final_norm_g):
    x_prompt = np.asarray(x_prompt, dtype=np.float32)
    x_sample = np.asarray(x_sample, dtype=np.float32)
    NBLK, SLOT = 128, 16
    weights = dict(norm_g=norm_g, attn_w_in=attn_w_in, attn_sink=attn_sink, attn_w_out=attn_w_out,
                   mlstm_w_in=mlstm_w_in, mlstm_gate_bias=mlstm_gate_bias, mlstm_head_norm=mlstm_head_norm,
                   mlstm_w_out=mlstm_w_out, final_norm_g=final_norm_g)
    counts = [6, 6, 5, 5, 5, 5]
    core_seqs = [[x_prompt[0]], [x_prompt[1]]]
    assign = []
    n0 = 0
    for c in counts:
        ids = list(range(n0, n0 + c))
        n0 += c
        assign.append(ids)
        seqs = [x_sample[k] for k in ids]
        while len(seqs) < 8:
            seqs.append(np.zeros((2048, D), np.float32))
        core_seqs.append(seqs)
    outs = run_cores(core_seqs, weights, NBLK, SLOT)
    y_prompt = np.stack([outs[0], outs[1]], axis=0).astype(np.float32)
    y_sample = np.empty((32, 2048, D), np.float32)
    for ci, ids in enumerate(assign):
        o = outs[2 + ci]
        for n, k in enumerate(ids):
            y_sample[k] = o[n * 2048:(n + 1) * 2048]
    return (y_prompt, y_sample)
```

```python
import math
from contextlib import ExitStack

import numpy as np
import ml_dtypes
import concourse.bass as bass
import concourse.mybir as mybir
from concourse.bass_utils import run_bass_kernel_spmd

F32 = mybir.dt.float32
BF16 = mybir.dt.bfloat16
AF = mybir.ActivationFunctionType
ALU = mybir.AluOpType

D = 1024
EPS = 1e-6
ATT_IN = 2560
ML_IN = 4112
NEG = -30000.0


class Prog:
    ENGS = ("pe", "act", "dve", "pool", "sp")

    def __init__(self, nc, stack):
        self.nc = nc
        self.stack = stack
        self.q = {e: [] for e in self.ENGS}
        self.cnt = {e: 0 for e in self.ENGS}
        self.waited = {e: {} for e in self.ENGS}
        self.res = {}
        self.sems = {}
        self.dcnt = {}
        self.alias = {}
        for e in ("pe", "act", "dve", "pool"):
            self.sems[("eng", e)] = stack.enter_context(nc.semaphore("s_" + e))

    def _sem(self, key):
        if key not in self.sems:
            self.sems[key] = self.stack.enter_context(nc_sem(self.nc, "d_%d" % len(self.sems)))
            self.dcnt[key] = 0
        return self.sems[key]

    def op(self, eng, fn, reads=(), writes=(), accum=False, inc=True, dma=None):
        need = {}
        reads = [self.alias.get(r, r) for r in reads]
        writes = [self.alias.get(w, w) for w in writes]
        if dma is not None:
            dma = self.alias.get(dma, dma)

        def add(tok):
            k, v = tok
            if eng == "pe" and k == ("eng", "pe"):
                return
            if need.get(k, 0) < v:
                need[k] = v

        for r in reads:
            st = self.res.get(r)
            if st:
                for tok in st[0].items():
                    add(tok)
        for w in writes:
            st = self.res.get(w)
            if st:
                if accum:
                    for tok in st[2].items():
                        add(tok)
                else:
                    for tok in st[0].items():
                        add(tok)
                for tok in st[1].items():
                    add(tok)
        wl = []
        wd = self.waited[eng]
        for k, v in need.items():
            if wd.get(k, 0) < v:
                wd[k] = v
                wl.append((self.sems[k], v))
        if dma is not None:
            key = ("dma", dma)
            sem = self._sem(key)
            self.dcnt[key] += 16
            tok = (key, self.dcnt[key])
            incinfo = (sem, 16)
        else:
            key = ("eng", eng)
            if inc:
                self.cnt[eng] += 1
                tok = (key, self.cnt[eng])
                incinfo = (self.sems[key], 1)
            else:
                tok = (key, self.cnt[eng] + 1)
                incinfo = None
        self.q[eng].append((wl, fn, incinfo))
        for r in reads:
            st = self.res.setdefault(r, ({}, {}, {}))
            if st[1].get(tok[0], 0) < tok[1]:
                st[1][tok[0]] = tok[1]
        for w in writes:
            st = self.res.setdefault(w, ({}, {}, {}))
            if accum:
                st[0][tok[0]] = max(st[0].get(tok[0], 0), tok[1])
            else:
                st[0].clear()
                st[1].clear()
                st[2].clear()
                st[2].update(need)
                st[0][tok[0]] = tok[1]

    def emit(self, eng, e):
        for wl, fn, incinfo in self.q[eng]:
            for sem, v in wl:
                e.wait_ge(sem, v)
            ins = fn(e)
            if incinfo is not None:
                ins.then_inc(incinfo[0], incinfo[1])

    def final_waits(self, e):
        for key, v in self.dcnt.items():
            e.wait_ge(self.sems[key], v)


def nc_sem(nc, name):
    return nc.semaphore(name)


def build_nc(NBLK, SLOT, depth=4, dbg=0):
    nc = bass.Bass("TRN2", target_bir_lowering=False)
    NTOK = NBLK * 128

    def dram_in(name, shape):
        return nc.dram_tensor(name, list(shape), F32, kind="ExternalInput").ap()

    xin = dram_in("x", [NTOK, D])
    tab = dram_in("tab", [NTOK, 128])
    rbias = dram_in("rbias", [128, 2 * NBLK])
    bmask = nc.dram_tensor("bmask", [128, 1024], BF16, kind="ExternalInput").ap()
    norm_g = dram_in("norm_g", [4, D])
    attn_w_in = dram_in("attn_w_in", [2, D, ATT_IN])
    attn_sink = dram_in("attn_sink", [2, 16])
    attn_w_out = dram_in("attn_w_out", [2, D, D])
    mlstm_w_in = dram_in("mlstm_w_in", [2, D, ML_IN])
    mlstm_gate_bias = dram_in("mlstm_gate_bias", [2, 16])
    mlstm_head_norm = dram_in("mlstm_head_norm", [2, D])
    mlstm_w_out = dram_in("mlstm_w_out", [2, D, D])
    final_norm_g = dram_in("final_norm_g", [1, D])
    y = nc.dram_tensor("y", [NTOK, D], F32, kind="ExternalOutput").ap()
    xA = nc.dram_tensor("xA", [NTOK, D], F32).ap()
    hB = nc.dram_tensor("hB", [NTOK, D], F32).ap()
    s_qk = nc.dram_tensor("s_qk", [NTOK, D], BF16).ap()
    s_kb = nc.dram_tensor("s_kb", [NTOK, 512], BF16).ap()
    s_v = nc.dram_tensor("s_v", [NTOK, D], BF16).ap()
    s_gz = nc.dram_tensor("s_gz", [NTOK, D], BF16).ap()
    s_gt = nc.dram_tensor("s_gt", [NTOK, 16], F32).ap()

    with ExitStack() as stack:
        P = Prog(nc, stack)
        T = {}

        def sb(name, shape, dt=F32):
            T[name] = stack.enter_context(nc.sbuf_tensor(name, list(shape), dt))
            return T[name]

        def ps(name, shape, dt=F32):
            T[name] = stack.enter_context(nc.psum_tensor(name, list(shape), dt))
            return T[name]

        pv = ps("pv", [128, 1024])
        mm = [ps("mm0", [128, 512]), ps("mm1", [128, 512])]
        stp = [ps("st0", [128, 512]), ps("st1", [128, 512])]
        tp = ps("tp", [128, 1024], BF16)
        sm = ps("sm", [128, 512])

        ident = sb("ident", [128, 128], BF16)
        mask_ge = sb("mask_ge", [128, 512], BF16)
        mask_le = sb("mask_le", [128, 512], BF16)
        bm = sb("bm", [128, 1024], BF16)
        Mgt = sb("Mgt", [128, 128], F32)
        Mlt = sb("Mlt", [128, 128], F32)
        ones = sb("ones", [128, 128], F32)
        gb = sb("gb", [128, 2, D], F32)
        hng = sb("hng", [128, D], F32)
        gbias = sb("gbias", [128, 2, 16], F32)
        esink = sb("esink", [128, 2, 16], F32)
        rb = sb("rb", [128, 2 * NBLK], F32)
        w_in = sb("w_in", [128, 8, ML_IN], BF16)
        w_out = sb("w_out", [128, 8, D], BF16)
        NXS = 3
        xs = [sb("xs%d" % i, [128, D]) for i in range(5)]
        tabs = [sb("tabs%d" % i, [128, 128]) for i in range(NXS)]
        junk = sb("junk", [128, D], BF16)
        stt = [sb("stt%d" % i, [128, 8]) for i in range(2)]
        xn = [sb("xn0", [128, D], BF16)] * 2
        xnT = [sb("xnT0", [128, 8, 128], BF16)] * 2
        P.alias.update({"xn1": "xn0", "xnT1": "xnT0", "xs5": "hbt0", "szs2": "vt"})
        qf = sb("qf", [128, D])
        kf = sb("kf", [128, 256])
        rt1 = sb("rt1", [128, D])
        rt2 = sb("rt2", [128, D])
        kt1 = sb("kt1", [128, 256])
        kt2 = sb("kt2", [128, 256])
        qr = sb("qr", [128, D], BF16)
        kr = sb("kr", [128, 4, 2, 128], BF16)
        NQ = 2
        qT = [sb("qT%d" % i, [128, 8, 128], BF16) for i in range(NQ)]
        NK = 3
        kTd = [sb("kTd%d" % i, [128, 8, 128], BF16) for i in range(NK)]
        vaug = [sb("vaug%d" % i, [128, 4, 72], BF16) for i in range(4)]
        szs = [sb("szs%d" % i, [128, D], BF16) for i in range(NQ)]
        NPT = 4
        PT = [sb("PT%d" % i, [128, 512], BF16) for i in range(NPT)]
        dn = sb("dn", [128, 8])
        rdn = sb("rdn", [128, 8])
        o1 = sb("o1", [128, 512])
        og = sb("og", [128, D], BF16)
        ogT = sb("ogT", [128, 8, 128], BF16)
        xo = [sb("xo%d" % i, [128, D]) for i in range(2)]
        wst = xo
        qb = sb("qb", [128, 512], BF16)
        kbs = [sb("kb0", [128, 512], BF16), sb("kb1", [128, 512], BF16)]
        gzb = [szs[0], szs[1]]
        P.alias.update({"gzb0": "szs0", "gzb1": "szs1", "wst0": "xo0", "wst1": "xo1"})
        vf = sb("vf", [128, 4, 260])
        so = qf
        szm = rt1
        P.alias.update({"so": "qf", "szm": "rt1", "h1": "rt2", "hsum": "rt2", "hn": "rt2", "gz": "rt1", "gz2": "qf",
                        "qkT": "qT0", "yo0": "qf", "yo1": "rt1", "hbl0": "hbt0", "hbl1": "hbt1"})
        gt = sb("gt", [128, 16])
        gt3 = [sb("gt3_%d" % i, [128, 16]) for i in range(3)]
        qkT = qT[0]
        gsms = [sb("gsm0", [128, 8, 4]), sb("gsm1", [128, 8, 4])]
        osm = sb("osm", [128, 8, 4])
        vt = sb("vt", [128, 4, 260], BF16)
        Cst = sb("Cst", [128, 4, 260])
        cbf = sb("cbf", [128, 4, 260], BF16)
        hbt = [sb("hbt%d" % i, [128, D]) for i in range(2)]
        hbl = [hbt[0], hbt[1]]
        h1 = rt2
        hsum = rt2
        hn = rt2
        gz = rt1
        gz2 = qf
        yo = [qf, rt1]
        xsa = xs + [hbt[0]]
        szs.append(vt[:].rearrange("p a b -> p (a b)")[:, 0:1024])

        def affine(tile_ap, pattern, cmp, name, cm=1):
            P.op("pool", lambda e: e.memset(tile_ap, 1.0), writes=[name])
            P.op("pool", lambda e: e.affine_select(out=tile_ap, in_=tile_ap, pattern=pattern, compare_op=cmp,
                                                   fill=0.0, base=0, channel_multiplier=cm),
                 reads=[name], writes=[name])

        affine(ident[:], [[-1, 128]], ALU.is_equal, "ident")
        affine(mask_ge[:].rearrange("p (a b) -> p a b", a=4), [[0, 4], [-1, 128]], ALU.is_ge, "mask_ge")
        affine(mask_le[:].rearrange("p (a b) -> p a b", a=4), [[0, 4], [1, 128]], ALU.is_ge, "mask_le", cm=-1)
        affine(Mgt[:], [[-1, 128]], ALU.is_gt, "Mgt")
        affine(Mlt[:], [[1, 128]], ALU.is_gt, "Mlt", cm=-1)
        P.op("pool", lambda e: e.memset(ones[:], 1.0), writes=["ones"])
        for i in range(4):
            P.op("pool", lambda e, i=i: e.memset(vaug[i][:].rearrange("p a b -> p (a b)"), 1.0), writes=["vaug%d" % i])
        P.op("pool", lambda e: e.memset(vf[:], 1.0), writes=["vf"])
        P.op("pool", lambda e: e.memset(kr[:].rearrange("p a b c -> p (a b c)"), 0.0), writes=["kr"])

        P.op("sp", lambda e: e.dma_start(out=bm[:], in_=bmask[:, :]), writes=["bm"], dma="bm")
        P.op("sp", lambda e: e.dma_start(out=rb[:], in_=rbias[:, :]), writes=["rb"], dma="rb")
        P.op("sp", lambda e: e.dma_start(out=gb[:, 1, :], in_=final_norm_g[0, :].partition_broadcast(128)),
             writes=["gbf"], dma="gbf")
        for j in range(2):
            P.op("sp", lambda e, j=j: e.dma_start(out=gbias[:, j, :], in_=mlstm_gate_bias[j, :].partition_broadcast(128)),
                 writes=["gbias"], accum=True, dma="gbias")
            P.op("sp", lambda e, j=j: e.dma_start(out=esink[:, j, :], in_=attn_sink[j, :].partition_broadcast(128)),
                 writes=["esink"], accum=True, dma="esink")
        P.op("act", lambda e: e.activation(out=esink[:], in_=esink[:], func=AF.Exp), reads=["esink"], writes=["esink"])

        wctr = [0]

        def load_weight(dst, src2d, N, name):
            stg = [(xo[0], "xo0"), (xo[1], "xo1"), (qf, "qf"), (rt1, "rt1"), (rt2, "rt2"), (hbt[0], "hbt0"), (hbt[1], "hbt1")]
            npieces = (N + 1023) // 1024
            first = True
            for c in range(8):
                for pc in range(npieces):
                    n0 = pc * 1024
                    n1 = min(N, n0 + 1024)
                    st_t, st_n = stg[wctr[0] % len(stg)]
                    wctr[0] += 1
                    P.op("sp", lambda e, st_t=st_t, c=c, n0=n0, n1=n1: e.dma_start(
                        out=st_t[:, 0:n1 - n0], in_=src2d[c * 128:(c + 1) * 128, n0:n1]),
                        writes=[st_n], dma=st_n)
                    eng = ("act", "dve", "act", "dve", "pool")[wctr[0] % 5]
                    if eng == "act":
                        f = lambda e, st_t=st_t, c=c, n0=n0, n1=n1: e.activation(out=dst[:, c, n0:n1], in_=st_t[:, 0:n1 - n0], func=AF.Copy)
                    else:
                        f = lambda e, st_t=st_t, c=c, n0=n0, n1=n1: e.tensor_copy(out=dst[:, c, n0:n1], in_=st_t[:, 0:n1 - n0])
                    P.op(eng, f, reads=[st_n], writes=[name], accum=not first)
                    first = False

        def load_x(src, i, layer_tag, with_tab=False, with_hb=False):
            s = i % NXS
            P.op("sp", lambda e: e.dma_start(out=xs[s][:], in_=src[i * 128:(i + 1) * 128, :]),
                 reads=[(layer_tag, i)], writes=["xs%d" % s], dma="xs%d" % s)
            if with_tab:
                P.op("sp", lambda e: e.dma_start(out=tabs[s][:], in_=tab[i * 128:(i + 1) * 128, :]),
                     writes=["tabs%d" % s], dma="tabs%d" % s)
            if with_hb:
                hs_ = i % 3
                P.op("sp", lambda e: e.dma_start(out=hbl[hs_][:], in_=hB[i * 128:(i + 1) * 128, :]),
                     reads=[("hB", i)], writes=["hbl%d" % hs_], dma="hbl%d" % hs_)

        def rms_stats(src_tile, src_name, st, st_name, extra_scale=None):
            P.op("act", lambda e: e.activation(out=junk[:], in_=src_tile[:], func=AF.Square, accum_out=st[:, 0:1]),
                 reads=[src_name], writes=[st_name + "a", "junk"])
            P.op("act", lambda e: e.activation(out=st[:, 1:2], in_=st[:, 0:1], func=AF.Ln, scale=1.0 / D, bias=EPS),
                 reads=[st_name + "a"], writes=[st_name + "b"])
            P.op("act", lambda e: e.activation(out=st[:, 2:3], in_=st[:, 1:2], func=AF.Exp, scale=-0.5),
                 reads=[st_name + "b"], writes=[st_name])
            if extra_scale is not None:
                P.op("dve", lambda e: e.tensor_scalar_mul(out=st[:, 3:4], in0=st[:, 2:3], scalar1=float(extra_scale)),
                     reads=[st_name], writes=[st_name + "c"])

        def norm_and_transpose(i, layer, extra_scale):
            s = i % NXS
            p = i % 2
            st = stt[p]
            stn = "stt%d" % p
            rms_stats(xs[s], "xs%d" % s, st, stn, extra_scale)
            P.op("dve", lambda e: e.scalar_tensor_tensor(out=xn[p][:], in0=xs[s][:], scalar=st[:, 2:3], in1=gb[:, 0, :],
                                                         op0=ALU.mult, op1=ALU.mult),
                 reads=["xs%d" % s, stn, "gb"], writes=["xn%d" % p])
            for c in range(8):
                P.op("pe", lambda e, c=c: e.transpose(out=tp[:, c * 128:(c + 1) * 128], in_=xn[p][:, c * 128:(c + 1) * 128],
                                                      identity=ident[:]),
                     reads=["xn%d" % p, "ident"], writes=["tp"], accum=(c > 0), inc=(c == 7))
            P.op("dve", lambda e: e.tensor_copy(out=xnT[p][:].rearrange("p c t -> p (c t)"), in_=tp[:]),
                 reads=["tp"], writes=["xnT%d" % p])
            return s, p, st, stn

        mmctr = [0]

        def proj_piece(p, n0, n1):
            b = mmctr[0] % 2
            mmctr[0] += 1
            for c in range(8):
                P.op("pe", lambda e, c=c: e.matmul(mm[b][:, 0:n1 - n0], lhsT=xnT[p][:, c, :], rhs=w_in[:, c, n0:n1],
                                                   start=(c == 0), stop=(c == 7)),
                     reads=["xnT%d" % p, "w_in"], writes=["mm%d" % b], accum=(c > 0), inc=(c == 7))
            return b

        def out_proj_and_store(i, s, srcT, srcT_name, dst, dst_tag, final_norm, xsl=None):
            xsl = xsl or xs
            ob = i % 2
            for half in range(2):
                b = mmctr[0] % 2
                mmctr[0] += 1
                for c in range(8):
                    P.op("pe", lambda e, c=c, b=b, half=half: e.matmul(mm[b][:], lhsT=srcT[:, c, :],
                                                                      rhs=w_out[:, c, half * 512:(half + 1) * 512],
                                                                      start=(c == 0), stop=(c == 7)),
                         reads=[srcT_name, "w_out"], writes=["mm%d" % b], accum=(c > 0), inc=(c == 7))
                P.op("dve", lambda e, b=b, half=half: e.tensor_tensor(out=xo[ob][:, half * 512:(half + 1) * 512],
                                                                      in0=mm[b][:], in1=xsl[s][:, half * 512:(half + 1) * 512],
                                                                      op=ALU.add),
                     reads=["mm%d" % b, "xs%d" % s], writes=["xo%d" % ob], accum=(half > 0))
            if not final_norm:
                P.op("sp", lambda e: e.dma_start(out=dst[i * 128:(i + 1) * 128, :], in_=xo[ob][:]),
                     reads=["xo%d" % ob], writes=[(dst_tag, i)], dma="xo%d" % ob)
            else:
                st = stt[ob]
                stn = "fst%d" % ob
                fstt = fst[ob]
                rms_stats(xo[ob], "xo%d" % ob, fstt, stn, None)
                P.op("dve", lambda e: e.scalar_tensor_tensor(out=yo[ob][:], in0=xo[ob][:], scalar=fstt[:, 2:3], in1=gb[:, 1, :],
                                                             op0=ALU.mult, op1=ALU.mult),
                     reads=["xo%d" % ob, stn, "gbf"], writes=["yo%d" % ob])
                P.op("sp", lambda e: e.dma_start(out=y[i * 128:(i + 1) * 128, :], in_=yo[ob][:]),
                     reads=["yo%d" % ob], writes=[("y", i)], dma="yo%d" % ob)

        fst = [sb("fst%d" % i, [128, 8]) for i in range(2)]

        def attn_layer(layer, j, src, src_tag, dst, dst_tag, final_norm):
            P.op("sp", lambda e: e.dma_start(out=gb[:, 0, :], in_=norm_g[layer, :].partition_broadcast(128)),
                 writes=["gb"], dma="gb")
            load_weight(w_in, attn_w_in[j], ATT_IN, "w_in")
            load_weight(w_out, attn_w_out[j], D, "w_out")
            AX = 6
            NV = 4
            NS = 3
            tpB = sm[:].bitcast(BF16)
            ptc = [0]
            stc = [0]

            def ok(b):
                return 0 <= b < NBLK

            def L(b):
                s = b % AX
                P.op("sp", lambda e: e.dma_start(out=xsa[s][:], in_=src[b * 128:(b + 1) * 128, :]),
                     reads=[(src_tag, b)], writes=["xs%d" % s], dma="xs%d" % s)

            def LT(b):
                s = b % 3
                P.op("sp", lambda e: e.dma_start(out=tabs[s][:], in_=tab[b * 128:(b + 1) * 128, :]),
                     writes=["tabs%d" % s], dma="tabs%d" % s)

            def N0(b):
                s = b % AX
                st = stt[b % 2]
                stn = "stt%d" % (b % 2)
                rms_stats(xsa[s], "xs%d" % s, st, stn, None)
                P.op("dve", lambda e: e.scalar_tensor_tensor(out=xn[0][:], in0=xsa[s][:], scalar=st[:, 2:3], in1=gb[:, 0, :],
                                                             op0=ALU.mult, op1=ALU.mult),
                     reads=["xs%d" % s, stn, "gb"], writes=["xn0"])

            def N1(b):
                for c in range(8):
                    P.op("pe", lambda e, c=c: e.transpose(out=tpB[:, c * 128:(c + 1) * 128], in_=xn[0][:, c * 128:(c + 1) * 128],
                                                          identity=ident[:]),
                         reads=["xn0", "ident"], writes=["sm"], accum=(c > 0), inc=(c == 7))
                P.op("dve", lambda e: e.tensor_copy(out=xnT[0][:].rearrange("p c t -> p (c t)"), in_=tpB),
                     reads=["sm"], writes=["xnT0"])

            def P_pieces(b):
                sq = b % NS
                sv = b % NV
                pcs = []

                def pq(h2):
                    bk = proj_piece(0, h2 * 512, (h2 + 1) * 512)
                    P.op("act", lambda e: e.activation(out=qf[:, h2 * 512:(h2 + 1) * 512], in_=mm[bk][:], func=AF.Copy, scale=0.125),
                         reads=["mm%d" % bk], writes=["qf"], accum=(h2 > 0))

                def pkv():
                    bk = proj_piece(0, 1024, 1536)
                    P.op("act", lambda e: e.activation(out=kf[:], in_=mm[bk][:, 0:256], func=AF.Copy),
                         reads=["mm%d" % bk], writes=["kf"])
                    P.op("act", lambda e: e.activation(out=vaug[sv][:, :, 0:64],
                                                       in_=mm[bk][:, 256:512].rearrange("p (g d) -> p g d", g=4), func=AF.Copy),
                         reads=["mm%d" % bk], writes=["vaug%d" % sv])

                def pz(h2):
                    bk = proj_piece(0, 1536 + h2 * 512, 2048 + h2 * 512)
                    P.op("act", lambda e: e.activation(out=szs[sq][:, h2 * 512:(h2 + 1) * 512], in_=mm[bk][:], func=AF.Silu),
                         reads=["mm%d" % bk], writes=["szs%d" % sq], accum=(h2 > 0))

                def rope():
                    tb = tabs[b % 3]
                    tn = "tabs%d" % (b % 3)
                    q3 = qf[:].rearrange("p (h d) -> p h d", h=16)
                    t13 = rt1[:].rearrange("p (h d) -> p h d", h=16)
                    t23 = rt2[:].rearrange("p (h d) -> p h d", h=16)
                    cosb = tb[:, 0:64].unsqueeze(1).to_broadcast([128, 16, 64])
                    nsin = tb[:, 64:96].unsqueeze(1).to_broadcast([128, 16, 32])
                    psin = tb[:, 96:128].unsqueeze(1).to_broadcast([128, 16, 32])
                    P.op("dve", lambda e: e.tensor_tensor(out=t13, in0=q3, in1=cosb, op=ALU.mult),
                         reads=["qf", tn], writes=["rt1"])
                    P.op("dve", lambda e: e.tensor_tensor(out=t23[:, :, 0:32], in0=q3[:, :, 32:64], in1=nsin, op=ALU.mult),
                         reads=["qf", tn], writes=["rt2"])
                    P.op("dve", lambda e: e.tensor_tensor(out=t23[:, :, 32:64], in0=q3[:, :, 0:32], in1=psin, op=ALU.mult),
                         reads=["qf", tn], writes=["rt2"], accum=True)
                    P.op("pool", lambda e: e.tensor_tensor(out=qr[:], in0=rt1[:], in1=rt2[:], op=ALU.add),
                         reads=["rt1", "rt2"], writes=["qr"])
                    k3 = kf[:].rearrange("p (h d) -> p h d", h=4)
                    kt13 = kt1[:].rearrange("p (h d) -> p h d", h=4)
                    kt23 = kt2[:].rearrange("p (h d) -> p h d", h=4)
                    cosk = tb[:, 0:64].unsqueeze(1).to_broadcast([128, 4, 64])
                    nsink = tb[:, 64:96].unsqueeze(1).to_broadcast([128, 4, 32])
                    psink = tb[:, 96:128].unsqueeze(1).to_broadcast([128, 4, 32])
                    P.op("dve", lambda e: e.tensor_tensor(out=kt13, in0=k3, in1=cosk, op=ALU.mult),
                         reads=["kf", tn], writes=["kt1"])
                    P.op("dve", lambda e: e.tensor_tensor(out=kt23[:, :, 0:32], in0=k3[:, :, 32:64], in1=nsink, op=ALU.mult),
                         reads=["kf", tn], writes=["kt2"])
                    P.op("dve", lambda e: e.tensor_tensor(out=kt23[:, :, 32:64], in0=k3[:, :, 0:32], in1=psink, op=ALU.mult),
                         reads=["kf", tn], writes=["kt2"], accum=True)
                    P.op("dve", lambda e: e.tensor_tensor(out=kr[:, :, 0, 0:64], in0=kt13, in1=kt23, op=ALU.add),
                         reads=["kt1", "kt2"], writes=["kr"])
                    P.op("dve", lambda e: e.tensor_tensor(out=kr[:, :, 1, 64:128], in0=kt13, in1=kt23, op=ALU.add),
                         reads=["kt1", "kt2"], writes=["kr"], accum=True)

                return [lambda: pq(0), lambda: pq(1), pkv, lambda: pz(0), lambda: pz(1), rope]

            def T(b):
                sq = b % 2
                sk = b % NK
                for c in range(8):
                    P.op("pe", lambda e, c=c: e.transpose(out=tpB[:, c * 128:(c + 1) * 128], in_=qr[:, c * 128:(c + 1) * 128],
                                                          identity=ident[:]),
                         reads=["qr", "ident"], writes=["sm"], accum=(c > 0), inc=(c == 7))
                P.op("dve", lambda e: e.tensor_copy(out=qT[sq][:].rearrange("p c t -> p (c t)"), in_=tpB),
                     reads=["sm"], writes=["qT%d" % sq])
                for g in range(8):
                    P.op("pe", lambda e, g=g: e.transpose(out=tp[:, g * 128:(g + 1) * 128], in_=kr[:, g // 2, g % 2, :], identity=ident[:]),
                         reads=["kr", "ident"], writes=["tp"], accum=(g > 0), inc=(g == 7))
                P.op("act", lambda e: e.activation(out=kTd[sk][:].rearrange("p c t -> p (c t)"), in_=tp[:], func=AF.Copy),
                     reads=["tp"], writes=["kTd%d" % sk])

            def S(i, pieces):
                sq = i % 2
                ss = i % NS
                nbrs = []
                if i > 0:
                    nbrs.append((i - 1, bm[:, 0:512] if (i % SLOT == 0) else mask_ge[:], "bm" if (i % SLOT == 0) else "mask_ge"))
                nbrs.append((i, None, None))
                if i < NBLK - 1:
                    nbrs.append((i + 1, bm[:, 512:1024] if ((i + 1) % SLOT == 0) else mask_le[:],
                                 "bm" if ((i + 1) % SLOT == 0) else "mask_le"))
                for half in range(2):
                    for gl in range(2):
                        g = half * 2 + gl
                        pts = []
                        for jn, (jb, mk, mkn) in enumerate(nbrs):
                            skj = jb % NK
                            svj = jb % NV
                            sb_ = stc[0] % 2
                            stc[0] += 1
                            for par in range(2):
                                P.op("pe", lambda e, par=par, sb_=sb_, skj=skj, g=g: e.matmul(
                                    stp[sb_][:, par * 256:(par + 1) * 256],
                                    lhsT=kTd[skj][:, g * 2 + par, :],
                                    rhs=qT[sq][:, 2 * g:2 * g + 2, :].rearrange("p c t -> p (c t)"),
                                    start=True, stop=True),
                                    reads=["kTd%d" % skj, "qT%d" % sq], writes=["st%d" % sb_], accum=(par > 0), inc=(par == 1))
                            pt = ptc[0] % NPT
                            ptc[0] += 1
                            P.op("act", lambda e, pt=pt, sb_=sb_: e.activation(out=PT[pt][:], in_=stp[sb_][:], func=AF.Exp),
                                 reads=["st%d" % sb_], writes=["PT%d" % pt])
                            if mk is not None:
                                P.op("dve", lambda e, pt=pt, mk=mk: e.tensor_tensor(out=PT[pt][:], in0=PT[pt][:], in1=mk, op=ALU.mult),
                                     reads=["PT%d" % pt, mkn], writes=["PT%d" % pt])
                            pts.append((pt, svj))
                        if pieces:
                            pieces.pop(0)()
                        for hh in range(4):
                            hl = gl * 4 + hh
                            for jn, (pt, svj) in enumerate(pts):
                                P.op("pe", lambda e, hh=hh, hl=hl, pt=pt, svj=svj, jn=jn, g=g, npts=len(pts): e.matmul(
                                    pv[:, hl * 128:hl * 128 + 66], lhsT=PT[pt][:, (0, 2, 1, 3)[hh] * 128:((0, 2, 1, 3)[hh] + 1) * 128],
                                    rhs=vaug[svj][:, g, 0:66], start=(jn == 0), stop=(jn == npts - 1)),
                                    reads=["PT%d" % pt, "vaug%d" % svj], writes=["pv"],
                                    accum=not (jn == 0 and hh == 0 and gl == 0), inc=(hh == 3 and jn == len(pts) - 1))
                    pv3 = pv[:].rearrange("p (h d) -> p h d", h=8)
                    P.op("dve", lambda e, half=half: e.tensor_tensor(out=dn[:], in0=pv3[:, :, 64], in1=esink[:, j, half * 8:(half + 1) * 8],
                                                                     op=ALU.add),
                         reads=["pv", "esink"], writes=["dn"])
                    P.op("dve", lambda e: e.reciprocal(out=rdn[:], in_=dn[:]), reads=["dn"], writes=["rdn"])
                    P.op("dve", lambda e: e.tensor_tensor(out=o1[:].rearrange("p (h d) -> p h d", h=8), in0=pv3[:, :, 0:64],
                                                          in1=rdn[:].unsqueeze(2).to_broadcast([128, 8, 64]), op=ALU.mult),
                         reads=["pv", "rdn"], writes=["o1"])
                    P.op("pool", lambda e, half=half: e.tensor_tensor(out=og[:, half * 512:(half + 1) * 512], in0=o1[:],
                                                                     in1=szs[ss][:, half * 512:(half + 1) * 512], op=ALU.mult),
                         reads=["o1", "szs%d" % ss], writes=["og"], accum=(half > 0))

            def O1(b):
                for c in range(8):
                    P.op("pe", lambda e, c=c: e.transpose(out=tp[:, c * 128:(c + 1) * 128], in_=og[:, c * 128:(c + 1) * 128],
                                                          identity=ident[:]),
                         reads=["og", "ident"], writes=["tp"], accum=(c > 0), inc=(c == 7))
                P.op("dve", lambda e: e.tensor_copy(out=ogT[:].rearrange("p c t -> p (c t)"), in_=tp[:]),
                     reads=["tp"], writes=["ogT"])

            def O2(b):
                out_proj_and_store(b, b % AX, ogT, "ogT", dst, dst_tag, final_norm, xsl=xsa)

            for i in range(-4, NBLK + 1):
                if ok(i + 4):
                    L(i + 4)
                if ok(i + 3):
                    LT(i + 3)
                if ok(i - 1):
                    O1(i - 1)
                if ok(i + 1):
                    T(i + 1)
                if ok(i + 2):
                    N1(i + 2)
                if ok(i - 1):
                    O2(i - 1)
                pieces = P_pieces(i + 2) if ok(i + 2) else []
                if ok(i):
                    S(i, pieces)
                while pieces:
                    pieces.pop(0)()
                if ok(i + 3):
                    N0(i + 3)

        def mlstm_layer(layer, j, src, src_tag, dst, dst_tag, final_norm):
            P.op("sp", lambda e: e.dma_start(out=gb[:, 0, :], in_=norm_g[layer, :].partition_broadcast(128)),
                 writes=["gb"], dma="gb")
            P.op("sp", lambda e: e.dma_start(out=hng[:], in_=mlstm_head_norm[j, :].partition_broadcast(128)),
                 writes=["hng"], dma="hng")
            load_weight(w_in, mlstm_w_in[j], ML_IN, "w_in")
            load_weight(w_out, mlstm_w_out[j], D, "w_out")
            stc = [0]
            MX = 5
            qk3 = [(qT[0], "qT0"), (qT[1], "qT1"), (kTd[0], "kTd0")]
            kb3 = [(kbs[0], "kb0"), (kbs[1], "kb1"), (PT[1], "PT1")]
            gz3 = [(szs[0], "szs0"), (szs[1], "szs1"), (qr, "qr")]
            v3 = [(xn[0][:], "xn0"), (xnT[0][:].rearrange("p c t -> p (c t)"), "xnT0"),
                  (kTd[1][:].rearrange("p c t -> p (c t)"), "kTd1")]

            def run_pass(mode):
                A = (mode == "A")
                dirf = A
                order = list(range(NBLK)) if dirf else list(range(NBLK - 1, -1, -1))
                ig0, fg0 = (0, 4) if dirf else (8, 12)
                Mx, Mn = (Mgt, "Mgt") if dirf else (Mlt, "Mlt")
                mk, mkn = (mask_le, "mask_le") if dirf else (mask_ge, "mask_ge")

                def ok(n):
                    return 0 <= n < NBLK

                def QK(n):
                    return (qkT, "qkT") if A else qk3[n % 3]

                def KB(n):
                    return (kbs[n % 2], "kb%d" % (n % 2)) if A else kb3[n % 3]

                def GT(n):
                    return (gt, "gt") if A else (gt3[n % 3], "gt3_%d" % (n % 3))

                def GZ(n):
                    return (gzb[n % 2], "gzb%d" % (n % 2)) if A else gz3[n % 3]

                def L(n):
                    i = order[n]
                    s = n % MX
                    P.op("sp", lambda e: e.dma_start(out=xs[s][:], in_=src[i * 128:(i + 1) * 128, :]),
                         reads=[(src_tag, i)], writes=["xs%d" % s], dma="xs%d" % s)

                def LB(n):
                    i = order[n]
                    r0, r1 = i * 128, (i + 1) * 128
                    L(n)
                    t, tn = qk3[n % 3]
                    P.op("sp", lambda e: e.dma_start(out=t[:].rearrange("p c t -> p (c t)"), in_=s_qk[r0:r1, :]),
                         reads=[("s_qk", i)], writes=[tn], dma=tn)
                    t2, tn2 = kb3[n % 3]
                    P.op("sp", lambda e: e.dma_start(out=t2[:], in_=s_kb[r0:r1, :]),
                         reads=[("s_kb", i)], writes=[tn2], dma=tn2)
                    t3, tn3 = gz3[n % 3]
                    P.op("sp", lambda e: e.dma_start(out=t3[:], in_=s_gz[r0:r1, :]),
                         reads=[("s_gz", i)], writes=[tn3], dma=tn3)
                    t4, tn4 = v3[n % 3]
                    P.op("sp", lambda e: e.dma_start(out=t4, in_=s_v[r0:r1, :]),
                         reads=[("s_v", i)], writes=[tn4], dma=tn4)
                    t5, tn5 = GT(n)
                    P.op("sp", lambda e: e.dma_start(out=t5[:], in_=s_gt[r0:r1, :]),
                         reads=[("s_gt", i)], writes=[tn5], dma=tn5)

                def LHB(n):
                    i = order[n]
                    hs_ = n % 2
                    P.op("sp", lambda e: e.dma_start(out=hbl[hs_][:], in_=hB[i * 128:(i + 1) * 128, :]),
                         reads=[("hB", i)], writes=["hbl%d" % hs_], dma="hbl%d" % hs_)

                def F0(n):
                    s = n % MX
                    p = n % 2
                    st = stt[p]
                    stn = "stt%d" % p
                    rms_stats(xs[s], "xs%d" % s, st, stn, None)
                    P.op("dve", lambda e: e.scalar_tensor_tensor(out=xn[0][:], in0=xs[s][:], scalar=st[:, 2:3], in1=gb[:, 0, :],
                                                                 op0=ALU.mult, op1=ALU.mult),
                         reads=["xs%d" % s, stn, "gb"], writes=["xn0"])

                def F1(n):
                    for c in range(8):
                        P.op("pe", lambda e, c=c: e.transpose(out=tp[:, c * 128:(c + 1) * 128], in_=xn[0][:, c * 128:(c + 1) * 128],
                                                              identity=ident[:]),
                             reads=["xn0", "ident"], writes=["tp"], accum=(c > 0), inc=(c == 7))
                    P.op("dve", lambda e: e.tensor_copy(out=xnT[0][:].rearrange("p c t -> p (c t)"), in_=tp[:]),
                         reads=["tp"], writes=["xnT0"])

                def F2(n):
                    i = order[n]
                    r0, r1 = i * 128, (i + 1) * 128
                    p = n % 2
                    kbt, kbn = KB(n)
                    b = proj_piece(0, 0, 512)
                    P.op("act", lambda e, b=b: e.activation(out=qb[:], in_=mm[b][:], func=AF.Copy),
                         reads=["mm%d" % b], writes=["qb"])
                    b = proj_piece(0, 512, 1024)
                    P.op("act", lambda e, b=b: e.activation(out=kbt[:], in_=mm[b][:], func=AF.Copy, scale=128.0 ** -0.5),
                         reads=["mm%d" % b], writes=[kbn])
                    P.op("sp", lambda e: e.dma_start(out=s_kb[r0:r1, :], in_=kbt[:]),
                         reads=[kbn], writes=[("s_kb", i)], dma=kbn)
                    for h2 in range(2):
                        b = proj_piece(0, 1024 + h2 * 512, 1536 + h2 * 512)
                        P.op("act", lambda e, b=b, h2=h2: e.activation(out=vf[:, 2 * h2:2 * h2 + 2, 0:256],
                                                                       in_=mm[b][:].rearrange("p (h d) -> p h d", h=2),
                                                                       func=AF.Copy),
                             reads=["mm%d" % b], writes=["vf"], accum=(h2 > 0))
                        P.op("act", lambda e, b=b, h2=h2: e.activation(out=qr[:, h2 * 512:(h2 + 1) * 512], in_=mm[b][:], func=AF.Copy),
                             reads=["mm%d" % b], writes=["qr"], accum=(h2 > 0))
                    P.op("sp", lambda e: e.dma_start(out=s_v[r0:r1, :], in_=qr[:]),
                         reads=["qr"], writes=[("s_v", i)], dma="qr")
                    b = proj_piece(0, 4096, 4112)
                    P.op("dve", lambda e, b=b: e.tensor_tensor(out=gt[:], in0=mm[b][:, 0:16], in1=gbias[:, j, :], op=ALU.add),
                         reads=["mm%d" % b, "gbias"], writes=["gt"])
                    P.op("sp", lambda e: e.dma_start(out=s_gt[r0:r1, :], in_=gt[:]),
                         reads=["gt"], writes=[("s_gt", i)], dma="gt")
                    gsm = gsms[p]
                    gs = "_%d" % p
                    P.op("act", lambda e: e.activation(out=gsm[:, 0, :], in_=gt[:, fg0:fg0 + 4], func=AF.Exp, scale=-1.0),
                         reads=["gt"], writes=["g_e1" + gs])
                    P.op("act", lambda e: e.activation(out=gsm[:, 1, :], in_=gsm[:, 0, :], func=AF.Ln, bias=1.0),
                         reads=["g_e1" + gs], writes=["g_l" + gs])
                    for h2 in range(2):
                        b = proj_piece(0, 2048 + h2 * 512, 2560 + h2 * 512)
                        P.op("act", lambda e, b=b, h2=h2: e.activation(out=so[:, h2 * 512:(h2 + 1) * 512], in_=mm[b][:],
                                                                       func=AF.Sigmoid),
                             reads=["mm%d" % b], writes=["so"], accum=(h2 > 0))
                    for h2 in range(2):
                        b = proj_piece(0, 3072 + h2 * 512, 3584 + h2 * 512)
                        P.op("act", lambda e, b=b, h2=h2: e.activation(out=szm[:, h2 * 512:(h2 + 1) * 512], in_=mm[b][:],
                                                                       func=AF.Silu),
                             reads=["mm%d" % b], writes=["szm"], accum=(h2 > 0))
                    P.op("pool", lambda e: e.tensor_tensor(out=szm[:], in0=so[:], in1=szm[:], op=ALU.mult),
                         reads=["so", "szm"], writes=["szm"])
                    P.op("pool", lambda e: e.tensor_tensor(out=gzb[p][:], in0=szm[:], in1=hng[:], op=ALU.mult),
                         reads=["szm", "hng"], writes=["gzb%d" % p])
                    P.op("sp", lambda e: e.dma_start(out=s_gz[r0:r1, :], in_=gzb[p][:]),
                         reads=["gzb%d" % p], writes=[("s_gz", i)], dma="gzb%d" % p)

                def F3a(n):
                    i = order[n]
                    r0, r1 = i * 128, (i + 1) * 128
                    p = n % 2
                    gsm = gsms[p]
                    gs = "_%d" % p
                    rbcol = i if dirf else NBLK + i
                    gtt, gtn = GT(n)
                    if A:
                        kbt, kbn = KB(n)
                        for c in range(8):
                            srct = qb if c < 4 else kbt
                            P.op("pe", lambda e, c=c, srct=srct: e.transpose(out=tp[:, c * 128:(c + 1) * 128],
                                                                              in_=srct[:, (c % 4) * 128:(c % 4 + 1) * 128], identity=ident[:]),
                                 reads=["qb", kbn, "ident"], writes=["tp"], accum=(c > 0), inc=(c == 7))
                        P.op("dve", lambda e: e.tensor_copy(out=qkT[:].rearrange("p c t -> p (c t)"), in_=tp[:]),
                             reads=["tp"], writes=["qkT"])
                        P.op("sp", lambda e: e.dma_start(out=s_qk[r0:r1, :], in_=qkT[:].rearrange("p c t -> p (c t)")),
                             reads=["qkT"], writes=[("s_qk", i)], dma="qkT")
                    else:
                        P.op("act", lambda e: e.activation(out=gsm[:, 0, :], in_=gtt[:, fg0:fg0 + 4], func=AF.Exp, scale=-1.0),
                             reads=[gtn], writes=["g_e1" + gs])
                        P.op("act", lambda e: e.activation(out=gsm[:, 1, :], in_=gsm[:, 0, :], func=AF.Ln, bias=1.0),
                             reads=["g_e1" + gs], writes=["g_l" + gs])
                    P.op("pe", lambda e: e.matmul(sm[:, 0:4], lhsT=Mx[:], rhs=gsm[:, 1, :], start=True, stop=True),
                         reads=[Mn, "g_l" + gs], writes=["sm"])
                    P.op("pe", lambda e: e.matmul(sm[:, 4:8], lhsT=ones[:], rhs=gsm[:, 1, :], start=True, stop=True),
                         reads=["ones", "g_l" + gs], writes=["sm"], accum=True)
                    P.op("dve", lambda e: e.tensor_tensor(out=gsm[:, 2, :], in0=gtt[:, ig0:ig0 + 4], in1=sm[:, 0:4], op=ALU.subtract),
                         reads=[gtn, "sm"], writes=["g_ta" + gs])
                    P.op("act", lambda e: e.activation(out=gsm[:, 3, :], in_=gsm[:, 2, :], func=AF.Exp),
                         reads=["g_ta" + gs], writes=["g_ea" + gs])
                    P.op("act", lambda e: e.activation(out=gsm[:, 4, :], in_=sm[:, 0:4], func=AF.Exp),
                         reads=["sm"], writes=["g_eo" + gs])
                    P.op("act", lambda e: e.activation(out=gsm[:, 5, :], in_=sm[:, 4:8], func=AF.Exp, scale=-1.0,
                                                       bias=rb[:, rbcol:rbcol + 1]),
                         reads=["sm", "rb"], writes=["g_dec" + gs])

                def F3b(n):
                    p = n % 2
                    gsm = gsms[p]
                    gs = "_%d" % p
                    if A:
                        P.op("dve", lambda e: e.tensor_tensor(out=vt[:], in0=vf[:],
                                                              in1=gsm[:, 3, :].unsqueeze(2).to_broadcast([128, 4, 260]), op=ALU.mult),
                             reads=["vf", "g_ea" + gs], writes=["vt"])
                    else:
                        vb, vbn = v3[n % 3]
                        P.op("dve", lambda e: e.tensor_tensor(out=vt[:, :, 0:256], in0=vb.rearrange("p (h d) -> p h d", h=4),
                                                              in1=gsm[:, 3, :].unsqueeze(2).to_broadcast([128, 4, 256]), op=ALU.mult),
                             reads=[vbn, "g_ea" + gs], writes=["vt"])
                        P.op("dve", lambda e: e.tensor_copy(out=vt[:, :, 256:260],
                                                            in_=gsm[:, 3, :].unsqueeze(2).to_broadcast([128, 4, 4])),
                             reads=["g_ea" + gs], writes=["vt"], accum=True)

                def F4(n):
                    qk, qkn = QK(n)
                    sb_ = stc[0] % 2
                    stc[0] += 1
                    for h in range(4):
                        P.op("pe", lambda e, h=h: e.matmul(stp[sb_][:, h * 128:(h + 1) * 128], lhsT=qk[:, 4 + h, :], rhs=qk[:, h, :],
                                                           start=True, stop=True),
                             reads=[qkn], writes=["st%d" % sb_], accum=(h > 0), inc=(h == 3))
                    P.op("dve", lambda e: e.tensor_tensor(out=PT[0][:], in0=stp[sb_][:], in1=mk[:], op=ALU.mult),
                         reads=["st%d" % sb_, mkn], writes=["PT0"])

                def B1a(n):
                    gsm = gsms[n % 2]
                    gs = "_%d" % (n % 2)
                    for h in range(4):
                        P.op("act", lambda e, h=h: e.activation(out=cbf[:, h, :], in_=Cst[:, h, :], func=AF.Copy, scale=gsm[:, 5, h:h + 1]),
                             reads=["Cst", "g_dec" + gs], writes=["cbf"], accum=(h > 0))

                def B1b(n):
                    qk, qkn = QK(n)
                    for h in range(4):
                        P.op("pe", lambda e, h=h: e.matmul(pv[:, h * 256:(h + 1) * 256], lhsT=PT[0][:, h * 128:(h + 1) * 128],
                                                           rhs=vt[:, h, 0:256], start=True, stop=False),
                             reads=["PT0", "vt"], writes=["pv"], accum=(h > 0), inc=False)
                        P.op("pe", lambda e, h=h: e.matmul(pv[:, h * 256:(h + 1) * 256], lhsT=qk[:, h, :],
                                                           rhs=cbf[:, h, 0:256], start=False, stop=True),
                             reads=[qkn, "cbf"], writes=["pv"], accum=True, inc=False)
                        P.op("pe", lambda e, h=h: e.matmul(sm[:, 16 + 2 * h:18 + 2 * h], lhsT=PT[0][:, h * 128:(h + 1) * 128],
                                                           rhs=vt[:, h, 256:258], start=True, stop=False),
                             reads=["PT0", "vt"], writes=["sm"], accum=True, inc=False)
                        P.op("pe", lambda e, h=h: e.matmul(sm[:, 16 + 2 * h:18 + 2 * h], lhsT=qk[:, h, :],
                                                           rhs=cbf[:, h, 256:258], start=False, stop=True),
                             reads=[qkn, "cbf"], writes=["sm"], accum=True, inc=(h == 3))

                def B2(n):
                    i = order[n]
                    p = n % 2
                    gsm = gsms[p]
                    gs = "_%d" % p
                    kbt, kbn = KB(n)
                    for h in range(4):
                        sb2 = stc[0] % 2
                        stc[0] += 1
                        P.op("pe", lambda e, h=h, sb2=sb2: e.matmul(stp[sb2][:, 0:258], lhsT=kbt[:, h * 128:(h + 1) * 128], rhs=vt[:, h, 0:258],
                                                                    start=True, stop=True),
                             reads=[kbn, "vt"], writes=["st%d" % sb2])
                        P.op("dve", lambda e, h=h, sb2=sb2: e.scalar_tensor_tensor(out=Cst[:, h, 0:258], in0=Cst[:, h, 0:258],
                                                                                   scalar=gsm[:, 5, h:h + 1],
                                                                                   in1=stp[sb2][:, 0:258], op0=ALU.mult, op1=ALU.add),
                             reads=["Cst", "g_dec" + gs, "st%d" % sb2], writes=["Cst"], accum=(h > 0))
                    P.op("dve", lambda e: e.tensor_tensor(out=osm[:, 0, :], in0=sm[:, 16:24].rearrange("p (h t) -> p h t", t=2)[:, :, 0],
                                                          in1=gsm[:, 4, :], op=ALU.mult),
                         reads=["sm", "g_eo" + gs], writes=["o_a1"])
                    P.op("dve", lambda e: e.scalar_tensor_tensor(out=osm[:, 1, :], in0=osm[:, 0, :], scalar=-1.0, in1=osm[:, 0, :],
                                                                 op0=ALU.mult, op1=ALU.max),
                         reads=["o_a1"], writes=["o_a2"])
                    P.op("dve", lambda e: e.tensor_scalar_max(out=osm[:, 2, :], in0=osm[:, 1, :], scalar1=1.0),
                         reads=["o_a2"], writes=["o_a3"])
                    P.op("dve", lambda e: e.reciprocal(out=osm[:, 3, :], in_=osm[:, 2, :]), reads=["o_a3"], writes=["o_ra"])
                    P.op("dve", lambda e: e.tensor_tensor(out=osm[:, 4, :], in0=osm[:, 3, :], in1=gsm[:, 4, :], op=ALU.mult),
                         reads=["o_ra", "g_eo" + gs], writes=["o_hs"])
                    pv3 = pv[:].rearrange("p (h d) -> p h d", h=4)
                    hsb = osm[:, 4, :].unsqueeze(2).to_broadcast([128, 4, 256])
                    if A:
                        hb_ = n % 2
                        P.op("dve", lambda e: e.tensor_tensor(out=hbt[hb_][:].rearrange("p (h d) -> p h d", h=4), in0=pv3, in1=hsb, op=ALU.mult),
                             reads=["pv", "o_hs"], writes=["hbt%d" % hb_])
                        P.op("sp", lambda e: e.dma_start(out=hB[i * 128:(i + 1) * 128, :], in_=hbt[hb_][:]),
                             reads=["hbt%d" % hb_], writes=[("hB", i)], dma="hbt%d" % hb_)
                    else:
                        hs_ = n % 2
                        for h in range(4):
                            P.op("dve", lambda e, h=h: e.scalar_tensor_tensor(out=hsum[:, h * 256:(h + 1) * 256], in0=pv[:, h * 256:(h + 1) * 256],
                                                                              scalar=osm[:, 4, h:h + 1], in1=hbl[hs_][:, h * 256:(h + 1) * 256],
                                                                              op0=ALU.mult, op1=ALU.add),
                                 reads=["pv", "o_hs", "hbl%d" % hs_], writes=["hsum"], accum=(h > 0))

                def B3v1(n):
                    hs_ = n % 2
                    for h in range(4):
                        P.op("act", lambda e, h=h: e.activation(out=junk[:, 0:256], in_=hsum[:, h * 256:(h + 1) * 256], func=AF.Square,
                                                                accum_out=osm[:, 5, h:h + 1]),
                             reads=["hsum"], writes=["o_hss", "junk"])
                    P.op("act", lambda e: e.activation(out=osm[:, 6, :], in_=osm[:, 5, :], func=AF.Ln, scale=1.0 / 256, bias=EPS),
                         reads=["o_hss"], writes=["o_hln"])
                    P.op("act", lambda e: e.activation(out=osm[:, 7, :], in_=osm[:, 6, :], func=AF.Exp, scale=-0.5),
                         reads=["o_hln"], writes=["o_hr"])

                def B3v2(n):
                    gzt, gzn = GZ(n)
                    P.op("dve", lambda e: e.tensor_tensor(out=hn[:].rearrange("p (h d) -> p h d", h=4),
                                                          in0=hsum[:].rearrange("p (h d) -> p h d", h=4),
                                                          in1=osm[:, 7, :].unsqueeze(2).to_broadcast([128, 4, 256]), op=ALU.mult),
                         reads=["hsum", "o_hr"], writes=["hn"])
                    P.op("pool", lambda e: e.tensor_tensor(out=og[:], in0=hn[:], in1=gzt[:], op=ALU.mult),
                         reads=["hn", gzn], writes=["og"])

                def B3pe(n):
                    bk = mmctr[0] % 2
                    mmctr[0] += 1
                    mmb = mm[bk][:].bitcast(BF16)
                    for c in range(8):
                        P.op("pe", lambda e, c=c: e.transpose(out=mmb[:, c * 128:(c + 1) * 128], in_=og[:, c * 128:(c + 1) * 128],
                                                              identity=ident[:]),
                             reads=["og", "ident"], writes=["mm%d" % bk], accum=(c > 0), inc=(c == 7))
                    P.op("dve", lambda e: e.tensor_copy(out=ogT[:].rearrange("p c t -> p (c t)"), in_=mmb),
                         reads=["mm%d" % bk], writes=["ogT"])

                def B4(n):
                    out_proj_and_store(order[n], n % MX, ogT, "ogT", dst, dst_tag, final_norm)

                P.op("pool", lambda e: e.memset(Cst[:].rearrange("p a b -> p (a b)"), 0.0), writes=["Cst"])
                if A:
                    for n in range(min(3, NBLK)):
                        L(n)
                    for n in range(-2, NBLK + 1):
                        if n + 3 >= 3 and ok(n + 3):
                            L(n + 3)
                        if ok(n + 1):
                            F1(n + 1)
                        if ok(n + 2):
                            F0(n + 2)
                        if ok(n + 1):
                            F2(n + 1)
                        if ok(n):
                            B1b(n)
                        if ok(n + 1):
                            F3a(n + 1)
                        if ok(n):
                            B2(n)
                        if ok(n + 1):
                            F4(n + 1)
                            B1a(n + 1)
                            F3b(n + 1)
                else:
                    for n in range(min(2, NBLK)):
                        LB(n)
                    LHB(0)
                    for n in range(-1, NBLK + 1):
                        if n + 2 >= 2 and ok(n + 2):
                            LB(n + 2)
                        if n + 1 >= 1 and ok(n + 1):
                            LHB(n + 1)
                        if ok(n):
                            B1b(n)
                            B2(n)
                        if ok(n + 1):
                            F3a(n + 1)
                        if ok(n - 1):
                            B3pe(n - 1)
                        if ok(n + 1):
                            F3b(n + 1)
                            F4(n + 1)
                            B1a(n + 1)
                        if ok(n):
                            B3v1(n)
                        if ok(n - 1):
                            B4(n - 1)
                        if ok(n):
                            B3v2(n)

            run_pass("A")
            run_pass("B")

        if dbg == 1:
            depth = 0
            P.op("sp", lambda e: e.dma_start(out=y[0:128, :], in_=gb[:, 1, :]), reads=["gbf"], writes=[("y", 0)], dma="dbg")
        if dbg == 2:
            depth = 0
            load_weight(w_in, attn_w_in[0], ATT_IN, "w_in")
            load_x(xin, 0, "xin", with_tab=True)
            P.op("sp", lambda e: e.dma_start(out=gb[:, 0, :], in_=norm_g[0, :].partition_broadcast(128)), writes=["gb"], dma="gb")
            norm_and_transpose(0, 0, None)
            b = proj_piece(0, 0, 512)
            P.op("act", lambda e: e.activation(out=xo[0][:, 0:512], in_=mm[b][:], func=AF.Copy), reads=["mm%d" % b], writes=["xo0"])
            P.op("sp", lambda e: e.dma_start(out=y[0:128, 0:512], in_=xo[0][:, 0:512]), reads=["xo0"], writes=[("y", 0)], dma="dbg")
        if dbg >= 3:
            depth = 1
        for layer in range(depth):
            src, src_tag = (xin, "xin") if layer == 0 else (xA, "xA")
            last = (layer == depth - 1)
            dst, dst_tag = (y, "y") if last else (xA, "xA")
            if layer % 2 == 0:
                attn_layer(layer, layer // 2, src, src_tag, dst, dst_tag, last)
            else:
                mlstm_layer(layer, layer // 2, src, src_tag, dst, dst_tag, last)

        with nc.Block() as block:
            @block.sync
            def _(e):
                P.emit("sp", e)
                P.final_waits(e)

            @block.scalar
            def _(e):
                P.emit("act", e)

            @block.vector
            def _(e):
                P.emit("dve", e)

            @block.gpsimd
            def _(e):
                P.emit("pool", e)

            @block.tensor
            def _(e):
                P.emit("pe", e)
        ninstr = {k: len(v) for k, v in P.q.items()}
    return nc, ninstr


ROPE_THETA = 10000.0


def _rope_tab(pos):
    half = 32
    inv = np.exp(np.float32(-math.log(ROPE_THETA)) * np.arange(half, dtype=np.float32) / np.float32(half)).astype(np.float32)
    ang = (pos.astype(np.float32)[:, None] * inv[None, :]).astype(np.float32)
    c = np.cos(ang.astype(np.float64)).astype(np.float32)
    s = np.sin(ang.astype(np.float64)).astype(np.float32)
    return np.concatenate([c, c, -s, s], axis=1).astype(np.float32)


def _core_inputs(seqs, NBLK, SLOT):
    NTOK = NBLK * 128
    x = np.zeros((NTOK, D), np.float32)
    whole = (len(seqs) == 1 and seqs[0].shape[0] == NTOK)
    if whole:
        x[:] = seqs[0]
        pos = np.arange(NTOK)
    else:
        L = SLOT * 128
        for n, sq in enumerate(seqs):
            assert sq.shape[0] == L
            x[n * L:(n + 1) * L] = sq
        pos = np.arange(NTOK) % L
    tab = _rope_tab(pos)
    rb = np.zeros((128, 2 * NBLK), np.float32)
    jj = np.arange(128)
    m_ge = (jj[:, None] >= jj[None, :]).astype(np.float32)
    m_le = (jj[:, None] <= jj[None, :]).astype(np.float32)
    bmask = np.zeros((128, 1024), np.float32)
    if whole:
        bmask[:, 0:512] = np.tile(m_ge, (1, 4))
        bmask[:, 512:1024] = np.tile(m_le, (1, 4))
    else:
        for i in range(NBLK):
            if i % SLOT == 0:
                rb[:, i] = NEG
            if (i + 1) % SLOT == 0:
                rb[:, NBLK + i] = NEG
    return {"x": x, "tab": tab, "rbias": rb, "bmask": bmask.astype(ml_dtypes.bfloat16)}


_CACHE = {}


def run_cores(core_seqs, weights, NBLK, SLOT, depth=4, dbg=0):
    key = (NBLK, SLOT, depth)
    if key not in _CACHE:
        _CACHE[key] = build_nc(NBLK, SLOT, depth, dbg)
    nc, _ = _CACHE[key]
    w = {k: np.ascontiguousarray(np.asarray(v, dtype=np.float32)) for k, v in weights.items()}
    w["final_norm_g"] = w["final_norm_g"].reshape(1, D)
    in_maps = []
    for seqs in core_seqs:
        m = _core_inputs(seqs, NBLK, SLOT)
        m.update(w)
        in_maps.append(m)
    res = run_bass_kernel_spmd(nc, in_maps, core_ids=list(range(len(core_seqs))))
    return [r["y"] for r in res.results]


def kernel(x_prompt, x_sample, norm_g, attn_w_in, attn_sink, attn_w_out, mlstm_w_in, mlstm_gate_bias,
           mlstm_head_norm, mlstm_w_out, final_norm_g):
    x_prompt = np.asarray(x_prompt, dtype=np.float32)
    x_sample = np.asarray(x_sample, dtype=np.float32)
    NBLK, SLOT = 128, 16
    weights = dict(norm_g=norm_g, attn_w_in=attn_w_in, attn_sink=attn_sink, attn_w_out=attn_w_out,
                   mlstm_w_in=mlstm_w_in, mlstm_gate_bias=mlstm_gate_bias, mlstm_head_norm=mlstm_head_norm,
                   mlstm_w_out=mlstm_w_out, final_norm_g=final_norm_g)
    counts = [6, 6, 5, 5, 5, 5]
    core_seqs = [[x_prompt[0]], [x_prompt[1]]]
    assign = []
    n0 = 0
    for c in counts:
        ids = list(range(n0, n0 + c))
        n0 += c
        assign.append(ids)
        seqs = [x_sample[k] for k in ids]
        while len(seqs) < 8:
            seqs.append(np.zeros((2048, D), np.float32))
        core_seqs.append(seqs)
    outs = run_cores(core_seqs, weights, NBLK, SLOT)
    y_prompt = np.stack([outs[0], outs[1]], axis=0).astype(np.float32)
    y_sample = np.empty((32, 2048, D), np.float32)
    for ci, ids in enumerate(assign):
        o = outs[2 + ci]
        for n, k in enumerate(ids):
            y_sample[k] = o[n * 2048:(n + 1) * 2048]
    return (y_prompt, y_sample)
```

```python
import math
from contextlib import ExitStack

import numpy as np
import ml_dtypes
import concourse.bass as bass
import concourse.mybir as mybir
from concourse.bass_utils import run_bass_kernel_spmd

F32 = mybir.dt.float32
BF16 = mybir.dt.bfloat16
AF = mybir.ActivationFunctionType
ALU = mybir.AluOpType

D = 1024
EPS = 1e-6
ATT_IN = 2560
ML_IN = 4112
NEG = -30000.0


class Prog:
    ENGS = ("pe", "act", "dve", "pool", "sp")

    def __init__(self, nc, stack):
        self.nc = nc
        self.stack = stack
        self.q = {e: [] for e in self.ENGS}
        self.cnt = {e: 0 for e in self.ENGS}
        self.waited = {e: {} for e in self.ENGS}
        self.res = {}
        self.sems = {}
        self.dcnt = {}
        self.alias = {}
        for e in ("pe", "act", "dve", "pool"):
            self.sems[("eng", e)] = stack.enter_context(nc.semaphore("s_" + e))

    def _sem(self, key):
        if key not in self.sems:
            self.sems[key] = self.stack.enter_context(nc_sem(self.nc, "d_%d" % len(self.sems)))
            self.dcnt[key] = 0
        return self.sems[key]

    def op(self, eng, fn, reads=(), writes=(), accum=False, inc=True, dma=None):
        need = {}
        reads = [self.alias.get(r, r) for r in reads]
        writes = [self.alias.get(w, w) for w in writes]
        if dma is not None:
            dma = self.alias.get(dma, dma)

        def add(tok):
            k, v = tok
            if eng == "pe" and k == ("eng", "pe"):
                return
            if need.get(k, 0) < v:
                need[k] = v

        for r in reads:
            st = self.res.get(r)
            if st:
                for tok in st[0].items():
                    add(tok)
        for w in writes:
            st = self.res.get(w)
            if st:
                if accum:
                    for tok in st[2].items():
                        add(tok)
                else:
                    for tok in st[0].items():
                        add(tok)
                for tok in st[1].items():
                    add(tok)
        wl = []
        wd = self.waited[eng]
        for k, v in need.items():
            if wd.get(k, 0) < v:
                wd[k] = v
                wl.append((self.sems[k], v))
        if dma is not None:
            key = ("dma", dma)
            sem = self._sem(key)
            self.dcnt[key] += 16
            tok = (key, self.dcnt[key])
            incinfo = (sem, 16)
        else:
            key = ("eng", eng)
            if inc:
                self.cnt[eng] += 1
                tok = (key, self.cnt[eng])
                incinfo = (self.sems[key], 1)
            else:
                tok = (key, self.cnt[eng] + 1)
                incinfo = None
        self.q[eng].append((wl, fn, incinfo))
        for r in reads:
            st = self.res.setdefault(r, ({}, {}, {}))
            if st[1].get(tok[0], 0) < tok[1]:
                st[1][tok[0]] = tok[1]
        for w in writes:
            st = self.res.setdefault(w, ({}, {}, {}))
            if accum:
                st[0][tok[0]] = max(st[0].get(tok[0], 0), tok[1])
            else:
                st[0].clear()
                st[1].clear()
                st[2].clear()
                st[2].update(need)
                st[0][tok[0]] = tok[1]

    def emit(self, eng, e):
        for wl, fn, incinfo in self.q[eng]:
            for sem, v in wl:
                e.wait_ge(sem, v)
            ins = fn(e)
            if incinfo is not None:
                ins.then_inc(incinfo[0], incinfo[1])

    def final_waits(self, e):
        for key, v in self.dcnt.items():
            e.wait_ge(self.sems[key], v)


def nc_sem(nc, name):
    return nc.semaphore(name)


def build_nc(NBLK, SLOT, depth=4, dbg=0):
    nc = bass.Bass("TRN2", target_bir_lowering=False)
    NTOK = NBLK * 128

    def dram_in(name, shape):
        return nc.dram_tensor(name, list(shape), F32, kind="ExternalInput").ap()

    xin = dram_in("x", [NTOK, D])
    tab = dram_in("tab", [NTOK, 128])
    rbias = dram_in("rbias", [128, 2 * NBLK])
    bmask = nc.dram_tensor("bmask", [128, 1024], BF16, kind="ExternalInput").ap()
    norm_g = dram_in("norm_g", [4, D])
    attn_w_in = dram_in("attn_w_in", [2, D, ATT_IN])
    attn_sink = dram_in("attn_sink", [2, 16])
    attn_w_out = dram_in("attn_w_out", [2, D, D])
    mlstm_w_in = dram_in("mlstm_w_in", [2, D, ML_IN])
    mlstm_gate_bias = dram_in("mlstm_gate_bias", [2, 16])
    mlstm_head_norm = dram_in("mlstm_head_norm", [2, D])
    mlstm_w_out = dram_in("mlstm_w_out", [2, D, D])
    final_norm_g = dram_in("final_norm_g", [1, D])
    y = nc.dram_tensor("y", [NTOK, D], F32, kind="ExternalOutput").ap()
    xA = nc.dram_tensor("xA", [NTOK, D], F32).ap()
    hB = nc.dram_tensor("hB", [NTOK, D], F32).ap()
    s_qk = nc.dram_tensor("s_qk", [NTOK, D], BF16).ap()
    s_kb = nc.dram_tensor("s_kb", [NTOK, 512], BF16).ap()
    s_v = nc.dram_tensor("s_v", [NTOK, D], BF16).ap()
    s_gz = nc.dram_tensor("s_gz", [NTOK, D], BF16).ap()
    s_gt = nc.dram_tensor("s_gt", [NTOK, 16], F32).ap()

    with ExitStack() as stack:
        P = Prog(nc, stack)
        T = {}

        def sb(name, shape, dt=F32):
            T[name] = stack.enter_context(nc.sbuf_tensor(name, list(shape), dt))
            return T[name]

        def ps(name, shape, dt=F32):
            T[name] = stack.enter_context(nc.psum_tensor(name, list(shape), dt))
            return T[name]

        pv = ps("pv", [128, 1024])
        mm = [ps("mm0", [128, 512]), ps("mm1", [128, 512])]
        stp = [ps("st0", [128, 512]), ps("st1", [128, 512])]
        tp = ps("tp", [128, 1024], BF16)
        sm = ps("sm", [128, 512])

        ident = sb("ident", [128, 128], BF16)
        mask_ge = sb("mask_ge", [128, 512], BF16)
        mask_le = sb("mask_le", [128, 512], BF16)
        bm = sb("bm", [128, 1024], BF16)
        Mgt = sb("Mgt", [128, 128], F32)
        Mlt = sb("Mlt", [128, 128], F32)
        ones = sb("ones", [128, 128], F32)
        gb = sb("gb", [128, 2, D], F32)
        hng = sb("hng", [128, D], F32)
        gbias = sb("gbias", [128, 2, 16], F32)
        esink = sb("esink", [128, 2, 16], F32)
        rb = sb("rb", [128, 2 * NBLK], F32)
        w_in = sb("w_in", [128, 8, ML_IN], BF16)
        w_out = sb("w_out", [128, 8, D], BF16)
        NXS = 3
        xs = [sb("xs%d" % i, [128, D]) for i in range(5)]
        tabs = [sb("tabs%d" % i, [128, 128]) for i in range(NXS)]
        junk = sb("junk", [128, D], BF16)
        stt = [sb("stt%d" % i, [128, 8]) for i in range(2)]
        xn = [sb("xn0", [128, D], BF16)] * 2
        xnT = [sb("xnT0", [128, 8, 128], BF16)] * 2
        P.alias.update({"xn1": "xn0", "xnT1": "xnT0", "xs5": "hbt0", "szs2": "vt"})
        qf = sb("qf", [128, D])
        kf = sb("kf", [128, 256])
        rt1 = sb("rt1", [128, D])
        rt2 = sb("rt2", [128, D])
        kt1 = sb("kt1", [128, 256])
        kt2 = sb("kt2", [128, 256])
        qr = sb("qr", [128, D], BF16)
        kr = sb("kr", [128, 4, 2, 128], BF16)
        NQ = 2
        qT = [sb("qT%d" % i, [128, 8, 128], BF16) for i in range(NQ)]
        NK = 3
        kTd = [sb("kTd%d" % i, [128, 8, 128], BF16) for i in range(NK)]
        vaug = [sb("vaug%d" % i, [128, 4, 72], BF16) for i in range(4)]
        szs = [sb("szs%d" % i, [128, D], BF16) for i in range(NQ)]
        NPT = 4
        PT = [sb("PT%d" % i, [128, 512], BF16) for i in range(NPT)]
        dn = sb("dn", [128, 8])
        rdn = sb("rdn", [128, 8])
        o1 = sb("o1", [128, 512])
        og = sb("og", [128, D], BF16)
        ogT = sb("ogT", [128, 8, 128], BF16)
        xo = [sb("xo%d" % i, [128, D]) for i in range(2)]
        wst = xo
        qb = sb("qb", [128, 512], BF16)
        kbs = [sb("kb0", [128, 512], BF16), sb("kb1", [128, 512], BF16)]
        gzb = [szs[0], szs[1]]
        P.alias.update({"gzb0": "szs0", "gzb1": "szs1", "wst0": "xo0", "wst1": "xo1"})
        vf = sb("vf", [128, 4, 260])
        so = qf
        szm = rt1
        P.alias.update({"so": "qf", "szm": "rt1", "h1": "rt2", "hsum": "rt2", "hn": "rt2", "gz": "rt1", "gz2": "qf",
                        "qkT": "qT0", "yo0": "qf", "yo1": "rt1", "hbl0": "hbt0", "hbl1": "hbt1"})
        gt = sb("gt", [128, 16])
        gt3 = [sb("gt3_%d" % i, [128, 16]) for i in range(3)]
        qkT = qT[0]
        gsms = [sb("gsm0", [128, 8, 4]), sb("gsm1", [128, 8, 4])]
        osm = sb("osm", [128, 8, 4])
        vt = sb("vt", [128, 4, 260], BF16)
        Cst = sb("Cst", [128, 4, 260])
        cbf = sb("cbf", [128, 4, 260], BF16)
        hbt = [sb("hbt%d" % i, [128, D]) for i in range(2)]
        hbl = [hbt[0], hbt[1]]
        h1 = rt2
        hsum = rt2
        hn = rt2
        gz = rt1
        gz2 = qf
        yo = [qf, rt1]
        xsa = xs + [hbt[0]]
        szs.append(vt[:].rearrange("p a b -> p (a b)")[:, 0:1024])

        def affine(tile_ap, pattern, cmp, name, cm=1):
            P.op("pool", lambda e: e.memset(tile_ap, 1.0), writes=[name])
            P.op("pool", lambda e: e.affine_select(out=tile_ap, in_=tile_ap, pattern=pattern, compare_op=cmp,
                                                   fill=0.0, base=0, channel_multiplier=cm),
                 reads=[name], writes=[name])

        affine(ident[:], [[-1, 128]], ALU.is_equal, "ident")
        affine(mask_ge[:].rearrange("p (a b) -> p a b", a=4), [[0, 4], [-1, 128]], ALU.is_ge, "mask_ge")
        affine(mask_le[:].rearrange("p (a b) -> p a b", a=4), [[0, 4], [1, 128]], ALU.is_ge, "mask_le", cm=-1)
        affine(Mgt[:], [[-1, 128]], ALU.is_gt, "Mgt")
        affine(Mlt[:], [[1, 128]], ALU.is_gt, "Mlt", cm=-1)
        P.op("pool", lambda e: e.memset(ones[:], 1.0), writes=["ones"])
        for i in range(4):
            P.op("pool", lambda e, i=i: e.memset(vaug[i][:].rearrange("p a b -> p (a b)"), 1.0), writes=["vaug%d" % i])
        P.op("pool", lambda e: e.memset(vf[:], 1.0), writes=["vf"])
        P.op("pool", lambda e: e.memset(kr[:].rearrange("p a b c -> p (a b c)"), 0.0), writes=["kr"])

        P.op("sp", lambda e: e.dma_start(out=bm[:], in_=bmask[:, :]), writes=["bm"], dma="bm")
        P.op("sp", lambda e: e.dma_start(out=rb[:], in_=rbias[:, :]), writes=["rb"], dma="rb")
        P.op("sp", lambda e: e.dma_start(out=gb[:, 1, :], in_=final_norm_g[0, :].partition_broadcast(128)),
             writes=["gbf"], dma="gbf")
        for j in range(2):
            P.op("sp", lambda e, j=j: e.dma_start(out=gbias[:, j, :], in_=mlstm_gate_bias[j, :].partition_broadcast(128)),
                 writes=["gbias"], accum=True, dma="gbias")
            P.op("sp", lambda e, j=j: e.dma_start(out=esink[:, j, :], in_=attn_sink[j, :].partition_broadcast(128)),
                 writes=["esink"], accum=True, dma="esink")
        P.op("act", lambda e: e.activation(out=esink[:], in_=esink[:], func=AF.Exp), reads=["esink"], writes=["esink"])

        wctr = [0]

        def load_weight(dst, src2d, N, name):
            stg = [(xo[0], "xo0"), (xo[1], "xo1"), (qf, "qf"), (rt1, "rt1"), (rt2, "rt2"), (hbt[0], "hbt0"), (hbt[1], "hbt1")]
            npieces = (N + 1023) // 1024
            first = True
            for c in range(8):
                for pc in range(npieces):
                    n0 = pc * 1024
                    n1 = min(N, n0 + 1024)
                    st_t, st_n = stg[wctr[0] % len(stg)]
                    wctr[0] += 1
                    P.op("sp", lambda e, st_t=st_t, c=c, n0=n0, n1=n1: e.dma_start(
                        out=st_t[:, 0:n1 - n0], in_=src2d[c * 128:(c + 1) * 128, n0:n1]),
                        writes=[st_n], dma=st_n)
                    eng = ("act", "dve", "act", "dve", "pool")[wctr[0] % 5]
                    if eng == "act":
                        f = lambda e, st_t=st_t, c=c, n0=n0, n1=n1: e.activation(out=dst[:, c, n0:n1], in_=st_t[:, 0:n1 - n0], func=AF.Copy)
                    else:
                        f = lambda e, st_t=st_t, c=c, n0=n0, n1=n1: e.tensor_copy(out=dst[:, c, n0:n1], in_=st_t[:, 0:n1 - n0])
                    P.op(eng, f, reads=[st_n], writes=[name], accum=not first)
                    first = False

        def load_x(src, i, layer_tag, with_tab=False, with_hb=False):
            s = i % NXS
            P.op("sp", lambda e: e.dma_start(out=xs[s][:], in_=src[i * 128:(i + 1) * 128, :]),
                 reads=[(layer_tag, i)], writes=["xs%d" % s], dma="xs%d" % s)
            if with_tab:
                P.op("sp", lambda e: e.dma_start(out=tabs[s][:], in_=tab[i * 128:(i + 1) * 128, :]),
                     writes=["tabs%d" % s], dma="tabs%d" % s)
            if with_hb:
                hs_ = i % 3
                P.op("sp", lambda e: e.dma_start(out=hbl[hs_][:], in_=hB[i * 128:(i + 1) * 128, :]),
                     reads=[("hB", i)], writes=["hbl%d" % hs_], dma="hbl%d" % hs_)

        def rms_stats(src_tile, src_name, st, st_name, extra_scale=None):
            P.op("act", lambda e: e.activation(out=junk[:], in_=src_tile[:], func=AF.Square, accum_out=st[:, 0:1]),
                 reads=[src_name], writes=[st_name + "a", "junk"])
            P.op("act", lambda e: e.activation(out=st[:, 1:2], in_=st[:, 0:1], func=AF.Ln, scale=1.0 / D, bias=EPS),
                 reads=[st_name + "a"], writes=[st_name + "b"])
            P.op("act", lambda e: e.activation(out=st[:, 2:3], in_=st[:, 1:2], func=AF.Exp, scale=-0.5),
                 reads=[st_name + "b"], writes=[st_name])
            if extra_scale is not None:
                P.op("dve", lambda e: e.tensor_scalar_mul(out=st[:, 3:4], in0=st[:, 2:3], scalar1=float(extra_scale)),
                     reads=[st_name], writes=[st_name + "c"])

        def norm_and_transpose(i, layer, extra_scale):
            s = i % NXS
            p = i % 2
            st = stt[p]
            stn = "stt%d" % p
            rms_stats(xs[s], "xs%d" % s, st, stn, extra_scale)
            P.op("dve", lambda e: e.scalar_tensor_tensor(out=xn[p][:], in0=xs[s][:], scalar=st[:, 2:3], in1=gb[:, 0, :],
                                                         op0=ALU.mult, op1=ALU.mult),
                 reads=["xs%d" % s, stn, "gb"], writes=["xn%d" % p])
            for c in range(8):
                P.op("pe", lambda e, c=c: e.transpose(out=tp[:, c * 128:(c + 1) * 128], in_=xn[p][:, c * 128:(c + 1) * 128],
                                                      identity=ident[:]),
                     reads=["xn%d" % p, "ident"], writes=["tp"], accum=(c > 0), inc=(c == 7))
            P.op("dve", lambda e: e.tensor_copy(out=xnT[p][:].rearrange("p c t -> p (c t)"), in_=tp[:]),
                 reads=["tp"], writes=["xnT%d" % p])
            return s, p, st, stn

        mmctr = [0]

        def proj_piece(p, n0, n1):
            b = mmctr[0] % 2
            mmctr[0] += 1
            for c in range(8):
                P.op("pe", lambda e, c=c: e.matmul(mm[b][:, 0:n1 - n0], lhsT=xnT[p][:, c, :], rhs=w_in[:, c, n0:n1],
                                                   start=(c == 0), stop=(c == 7)),
                     reads=["xnT%d" % p, "w_in"], writes=["mm%d" % b], accum=(c > 0), inc=(c == 7))
            return b

        def out_proj_and_store(i, s, srcT, srcT_name, dst, dst_tag, final_norm, xsl=None):
            xsl = xsl or xs
            ob = i % 2
            for half in range(2):
                b = mmctr[0] % 2
                mmctr[0] += 1
                for c in range(8):
                    P.op("pe", lambda e, c=c, b=b, half=half: e.matmul(mm[b][:], lhsT=srcT[:, c, :],
                                                                      rhs=w_out[:, c, half * 512:(half + 1) * 512],
                                                                      start=(c == 0), stop=(c == 7)),
                         reads=[srcT_name, "w_out"], writes=["mm%d" % b], accum=(c > 0), inc=(c == 7))
                P.op("dve", lambda e, b=b, half=half: e.tensor_tensor(out=xo[ob][:, half * 512:(half + 1) * 512],
                                                                      in0=mm[b][:], in1=xsl[s][:, half * 512:(half + 1) * 512],
                                                                      op=ALU.add),
                     reads=["mm%d" % b, "xs%d" % s], writes=["xo%d" % ob], accum=(half > 0))
            if not final_norm:
                P.op("sp", lambda e: e.dma_start(out=dst[i * 128:(i + 1) * 128, :], in_=xo[ob][:]),
                     reads=["xo%d" % ob], writes=[(dst_tag, i)], dma="xo%d" % ob)
            else:
                st = stt[ob]
                stn = "fst%d" % ob
                fstt = fst[ob]
                rms_stats(xo[ob], "xo%d" % ob, fstt, stn, None)
                P.op("dve", lambda e: e.scalar_tensor_tensor(out=yo[ob][:], in0=xo[ob][:], scalar=fstt[:, 2:3], in1=gb[:, 1, :],
                                                             op0=ALU.mult, op1=ALU.mult),
                     reads=["xo%d" % ob, stn, "gbf"], writes=["yo%d" % ob])
                P.op("sp", lambda e: e.dma_start(out=y[i * 128:(i + 1) * 128, :], in_=yo[ob][:]),
                     reads=["yo%d" % ob], writes=[("y", i)], dma="yo%d" % ob)

        fst = [sb("fst%d" % i, [128, 8]) for i in range(2)]

        def attn_layer(layer, j, src, src_tag, dst, dst_tag, final_norm):
            P.op("sp", lambda e: e.dma_start(out=gb[:, 0, :], in_=norm_g[layer, :].partition_broadcast(128)),
                 writes=["gb"], dma="gb")
            load_weight(w_in, attn_w_in[j], ATT_IN, "w_in")
            load_weight(w_out, attn_w_out[j], D, "w_out")
            AX = 6
            NV = 4
            NS = 3
            tpB = sm[:].bitcast(BF16)
            ptc = [0]
            stc = [0]

            def ok(b):
                return 0 <= b < NBLK

            def L(b):
                s = b % AX
                P.op("sp", lambda e: e.dma_start(out=xsa[s][:], in_=src[b * 128:(b + 1) * 128, :]),
                     reads=[(src_tag, b)], writes=["xs%d" % s], dma="xs%d" % s)

            def LT(b):
                s = b % 3
                P.op("sp", lambda e: e.dma_start(out=tabs[s][:], in_=tab[b * 128:(b + 1) * 128, :]),
                     writes=["tabs%d" % s], dma="tabs%d" % s)

            def N0(b):
                s = b % AX
                st = stt[b % 2]
                stn = "stt%d" % (b % 2)
                rms_stats(xsa[s], "xs%d" % s, st, stn, None)
                P.op("dve", lambda e: e.scalar_tensor_tensor(out=xn[0][:], in0=xsa[s][:], scalar=st[:, 2:3], in1=gb[:, 0, :],
                                                             op0=ALU.mult, op1=ALU.mult),
                     reads=["xs%d" % s, stn, "gb"], writes=["xn0"])

            def N1(b):
                for c in range(8):
                    P.op("pe", lambda e, c=c: e.transpose(out=tpB[:, c * 128:(c + 1) * 128], in_=xn[0][:, c * 128:(c + 1) * 128],
                                                          identity=ident[:]),
                         reads=["xn0", "ident"], writes=["sm"], accum=(c > 0), inc=(c == 7))
                P.op("dve", lambda e: e.tensor_copy(out=xnT[0][:].rearrange("p c t -> p (c t)"), in_=tpB),
                     reads=["sm"], writes=["xnT0"])

            def P_pieces(b):
                sq = b % NS
                sv = b % NV
                pcs = []

                def pq(h2):
                    bk = proj_piece(0, h2 * 512, (h2 + 1) * 512)
                    P.op("act", lambda e: e.activation(out=qf[:, h2 * 512:(h2 + 1) * 512], in_=mm[bk][:], func=AF.Copy, scale=0.125),
                         reads=["mm%d" % bk], writes=["qf"], accum=(h2 > 0))

                def pkv():
                    bk = proj_piece(0, 1024, 1536)
                    P.op("act", lambda e: e.activation(out=kf[:], in_=mm[bk][:, 0:256], func=AF.Copy),
                         reads=["mm%d" % bk], writes=["kf"])
                    P.op("act", lambda e: e.activation(out=vaug[sv][:, :, 0:64],
                                                       in_=mm[bk][:, 256:512].rearrange("p (g d) -> p g d", g=4), func=AF.Copy),
                         reads=["mm%d" % bk], writes=["vaug%d" % sv])

                def pz(h2):
                    bk = proj_piece(0, 1536 + h2 * 512, 2048 + h2 * 512)
                    P.op("act", lambda e: e.activation(out=szs[sq][:, h2 * 512:(h2 + 1) * 512], in_=mm[bk][:], func=AF.Silu),
                         reads=["mm%d" % bk], writes=["szs%d" % sq], accum=(h2 > 0))

                def rope():
                    tb = tabs[b % 3]
                    tn = "tabs%d" % (b % 3)
                    q3 = qf[:].rearrange("p (h d) -> p h d", h=16)
                    t13 = rt1[:].rearrange("p (h d) -> p h d", h=16)
                    t23 = rt2[:].rearrange("p (h d) -> p h d", h=16)
                    cosb = tb[:, 0:64].unsqueeze(1).to_broadcast([128, 16, 64])
                    nsin = tb[:, 64:96].unsqueeze(1).to_broadcast([128, 16, 32])
                    psin = tb[:, 96:128].unsqueeze(1).to_broadcast([128, 16, 32])
                    P.op("dve", lambda e: e.tensor_tensor(out=t13, in0=q3, in1=cosb, op=ALU.mult),
                         reads=["qf", tn], writes=["rt1"])
                    P.op("dve", lambda e: e.tensor_tensor(out=t23[:, :, 0:32], in0=q3[:, :, 32:64], in1=nsin, op=ALU.mult),
                         reads=["qf", tn], writes=["rt2"])
                    P.op("dve", lambda e: e.tensor_tensor(out=t23[:, :, 32:64], in0=q3[:, :, 0:32], in1=psin, op=ALU.mult),
                         reads=["qf", tn], writes=["rt2"], accum=True)
                    P.op("pool", lambda e: e.tensor_tensor(out=qr[:], in0=rt1[:], in1=rt2[:], op=ALU.add),
                         reads=["rt1", "rt2"], writes=["qr"])
                    k3 = kf[:].rearrange("p (h d) -> p h d", h=4)
                    kt13 = kt1[:].rearrange("p (h d) -> p h d", h=4)
                    kt23 = kt2[:].rearrange("p (h d) -> p h d", h=4)
                    cosk = tb[:, 0:64].unsqueeze(1).to_broadcast([128, 4, 64])
                    nsink = tb[:, 64:96].unsqueeze(1).to_broadcast([128, 4, 32])
                    psink = tb[:, 96:128].unsqueeze(1).to_broadcast([128, 4, 32])
                    P.op("dve", lambda e: e.tensor_tensor(out=kt13, in0=k3, in1=cosk, op=ALU.mult),
                         reads=["kf", tn], writes=["kt1"])
                    P.op("dve", lambda e: e.tensor_tensor(out=kt23[:, :, 0:32], in0=k3[:, :, 32:64], in1=nsink, op=ALU.mult),
                         reads=["kf", tn], writes=["kt2"])
                    P.op("dve", lambda e: e.tensor_tensor(out=kt23[:, :, 32:64], in0=k3[:, :, 0:32], in1=psink, op=ALU.mult),
                         reads=["kf", tn], writes=["kt2"], accum=True)
                    P.op("dve", lambda e: e.tensor_tensor(out=kr[:, :, 0, 0:64], in0=kt13, in1=kt23, op=ALU.add),
                         reads=["kt1", "kt2"], writes=["kr"])
                    P.op("dve", lambda e: e.tensor_tensor(out=kr[:, :, 1, 64:128], in0=kt13, in1=kt23, op=ALU.add),
                         reads=["kt1", "kt2"], writes=["kr"], accum=True)

                return [lambda: pq(0), lambda: pq(1), pkv, lambda: pz(0), lambda: pz(1), rope]

            def T(b):
                sq = b % 2
                sk = b % NK
                for c in range(8):
                    P.op("pe", lambda e, c=c: e.transpose(out=tpB[:, c * 128:(c + 1) * 128], in_=qr[:, c * 128:(c + 1) * 128],
                                                          identity=ident[:]),
                         reads=["qr", "ident"], writes=["sm"], accum=(c > 0), inc=(c == 7))
                P.op("dve", lambda e: e.tensor_copy(out=qT[sq][:].rearrange("p c t -> p (c t)"), in_=tpB),
                     reads=["sm"], writes=["qT%d" % sq])
                for g in range(8):
                    P.op("pe", lambda e, g=g: e.transpose(out=tp[:, g * 128:(g + 1) * 128], in_=kr[:, g // 2, g % 2, :], identity=ident[:]),
                         reads=["kr", "ident"], writes=["tp"], accum=(g > 0), inc=(g == 7))
                P.op("act", lambda e: e.activation(out=kTd[sk][:].rearrange("p c t -> p (c t)"), in_=tp[:], func=AF.Copy),
                     reads=["tp"], writes=["kTd%d" % sk])

            def S(i, pieces):
                sq = i % 2
                ss = i % NS
                nbrs = []
                if i > 0:
                    nbrs.append((i - 1, bm[:, 0:512] if (i % SLOT == 0) else mask_ge[:], "bm" if (i % SLOT == 0) else "mask_ge"))
                nbrs.append((i, None, None))
                if i < NBLK - 1:
                    nbrs.append((i + 1, bm[:, 512:1024] if ((i + 1) % SLOT == 0) else mask_le[:],
                                 "bm" if ((i + 1) % SLOT == 0) else "mask_le"))
                for half in range(2):
                    for gl in range(2):
                        g = half * 2 + gl
                        pts = []
                        for jn, (jb, mk, mkn) in enumerate(nbrs):
                            skj = jb % NK
                            svj = jb % NV
                            sb_ = stc[0] % 2
                            stc[0] += 1
                            for par in range(2):
                                P.op("pe", lambda e, par=par, sb_=sb_, skj=skj, g=g: e.matmul(
                                    stp[sb_][:, par * 256:(par + 1) * 256],
                                    lhsT=kTd[skj][:, g * 2 + par, :],
                                    rhs=qT[sq][:, 2 * g:2 * g + 2, :].rearrange("p c t -> p (c t)"),
                                    start=True, stop=True),
                                    reads=["kTd%d" % skj, "qT%d" % sq], writes=["st%d" % sb_], accum=(par > 0), inc=(par == 1))
                            pt = ptc[0] % NPT
                            ptc[0] += 1
                            P.op("act", lambda e, pt=pt, sb_=sb_: e.activation(out=PT[pt][:], in_=stp[sb_][:], func=AF.Exp),
                                 reads=["st%d" % sb_], writes=["PT%d" % pt])
                            if mk is not None:
                                P.op("dve", lambda e, pt=pt, mk=mk: e.tensor_tensor(out=PT[pt][:], in0=PT[pt][:], in1=mk, op=ALU.mult),
                                     reads=["PT%d" % pt, mkn], writes=["PT%d" % pt])
                            pts.append((pt, svj))
                        if pieces:
                            pieces.pop(0)()
                        for hh in range(4):
                            hl = gl * 4 + hh
                            for jn, (pt, svj) in enumerate(pts):
                                P.op("pe", lambda e, hh=hh, hl=hl, pt=pt, svj=svj, jn=jn, g=g, npts=len(pts): e.matmul(
                                    pv[:, hl * 128:hl * 128 + 66], lhsT=PT[pt][:, (0, 2, 1, 3)[hh] * 128:((0, 2, 1, 3)[hh] + 1) * 128],
                                    rhs=vaug[svj][:, g, 0:66], start=(jn == 0), stop=(jn == npts - 1)),
                                    reads=["PT%d" % pt, "vaug%d" % svj], writes=["pv"],
                                    accum=not (jn == 0 and hh == 0 and gl == 0), inc=(hh == 3 and jn == len(pts) - 1))
                    pv3 = pv[:].rearrange("p (h d) -> p h d", h=8)
                    P.op("dve", lambda e, half=half: e.tensor_tensor(out=dn[:], in0=pv3[:, :, 64], in1=esink[:, j, half * 8:(half + 1) * 8],
                                                                     op=ALU.add),
                         reads=["pv", "esink"], writes=["dn"])
                    P.op("dve", lambda e: e.reciprocal(out=rdn[:], in_=dn[:]), reads=["dn"], writes=["rdn"])
                    P.op("dve", lambda e: e.tensor_tensor(out=o1[:].rearrange("p (h d) -> p h d", h=8), in0=pv3[:, :, 0:64],
                                                          in1=rdn[:].unsqueeze(2).to_broadcast([128, 8, 64]), op=ALU.mult),
                         reads=["pv", "rdn"], writes=["o1"])
                    P.op("pool", lambda e, half=half: e.tensor_tensor(out=og[:, half * 512:(half + 1) * 512], in0=o1[:],
                                                                     in1=szs[ss][:, half * 512:(half + 1) * 512], op=ALU.mult),
                         reads=["o1", "szs%d" % ss], writes=["og"], accum=(half > 0))

            def O1(b):
                for c in range(8):
                    P.op("pe", lambda e, c=c: e.transpose(out=tp[:, c * 128:(c + 1) * 128], in_=og[:, c * 128:(c + 1) * 128],
                                                          identity=ident[:]),
                         reads=["og", "ident"], writes=["tp"], accum=(c > 0), inc=(c == 7))
                P.op("dve", lambda e: e.tensor_copy(out=ogT[:].rearrange("p c t -> p (c t)"), in_=tp[:]),
                     reads=["tp"], writes=["ogT"])

            def O2(b):
                out_proj_and_store(b, b % AX, ogT, "ogT", dst, dst_tag, final_norm, xsl=xsa)

            for i in range(-4, NBLK + 1):
                if ok(i + 4):
                    L(i + 4)
                if ok(i + 3):
                    LT(i + 3)
                if ok(i - 1):
                    O1(i - 1)
                if ok(i + 1):
                    T(i + 1)
                if ok(i + 2):
                    N1(i + 2)
                if ok(i - 1):
                    O2(i - 1)
                pieces = P_pieces(i + 2) if ok(i + 2) else []
                if ok(i):
                    S(i, pieces)
                while pieces:
                    pieces.pop(0)()
                if ok(i + 3):
                    N0(i + 3)

        def mlstm_layer(layer, j, src, src_tag, dst, dst_tag, final_norm):
            P.op("sp", lambda e: e.dma_start(out=gb[:, 0, :], in_=norm_g[layer, :].partition_broadcast(128)),
                 writes=["gb"], dma="gb")
            P.op("sp", lambda e: e.dma_start(out=hng[:], in_=mlstm_head_norm[j, :].partition_broadcast(128)),
                 writes=["hng"], dma="hng")
            load_weight(w_in, mlstm_w_in[j], ML_IN, "w_in")
            load_weight(w_out, mlstm_w_out[j], D, "w_out")
            stc = [0]
            MX = 5
            qk3 = [(qT[0], "qT0"), (qT[1], "qT1"), (kTd[0], "kTd0")]
            kb3 = [(kbs[0], "kb0"), (kbs[1], "kb1"), (PT[1], "PT1")]
            gz3 = [(szs[0], "szs0"), (szs[1], "szs1"), (qr, "qr")]
            v3 = [(xn[0][:], "xn0"), (xnT[0][:].rearrange("p c t -> p (c t)"), "xnT0"),
                  (kTd[1][:].rearrange("p c t -> p (c t)"), "kTd1")]

            def run_pass(mode):
                A = (mode == "A")
                dirf = A
                order = list(range(NBLK)) if dirf else list(range(NBLK - 1, -1, -1))
                ig0, fg0 = (0, 4) if dirf else (8, 12)
                Mx, Mn = (Mgt, "Mgt") if dirf else (Mlt, "Mlt")
                mk, mkn = (mask_le, "mask_le") if dirf else (mask_ge, "mask_ge")

                def ok(n):
                    return 0 <= n < NBLK

                def QK(n):
                    return (qkT, "qkT") if A else qk3[n % 3]

                def KB(n):
                    return (kbs[n % 2], "kb%d" % (n % 2)) if A else kb3[n % 3]

                def GT(n):
                    return (gt, "gt") if A else (gt3[n % 3], "gt3_%d" % (n % 3))

                def GZ(n):
                    return (gzb[n % 2], "gzb%d" % (n % 2)) if A else gz3[n % 3]

                def L(n):
                    i = order[n]
                    s = n % MX
                    P.op("sp", lambda e: e.dma_start(out=xs[s][:], in_=src[i * 128:(i + 1) * 128, :]),
                         reads=[(src_tag, i)], writes=["xs%d" % s], dma="xs%d" % s)

                def LB(n):
                    i = order[n]
                    r0, r1 = i * 128, (i + 1) * 128
                    L(n)
                    t, tn = qk3[n % 3]
                    P.op("sp", lambda e: e.dma_start(out=t[:].rearrange("p c t -> p (c t)"), in_=s_qk[r0:r1, :]),
                         reads=[("s_qk", i)], writes=[tn], dma=tn)
                    t2, tn2 = kb3[n % 3]
                    P.op("sp", lambda e: e.dma_start(out=t2[:], in_=s_kb[r0:r1, :]),
                         reads=[("s_kb", i)], writes=[tn2], dma=tn2)
                    t3, tn3 = gz3[n % 3]
                    P.op("sp", lambda e: e.dma_start(out=t3[:], in_=s_gz[r0:r1, :]),
                         reads=[("s_gz", i)], writes=[tn3], dma=tn3)
                    t4, tn4 = v3[n % 3]
                    P.op("sp", lambda e: e.dma_start(out=t4, in_=s_v[r0:r1, :]),
                         reads=[("s_v", i)], writes=[tn4], dma=tn4)
                    t5, tn5 = GT(n)
                    P.op("sp", lambda e: e.dma_start(out=t5[:], in_=s_gt[r0:r1, :]),
                         reads=[("s_gt", i)], writes=[tn5], dma=tn5)

                def LHB(n):
                    i = order[n]
                    hs_ = n % 2
                    P.op("sp", lambda e: e.dma_start(out=hbl[hs_][:], in_=hB[i * 128:(i + 1) * 128, :]),
                         reads=[("hB", i)], writes=["hbl%d" % hs_], dma="hbl%d" % hs_)

                def F0(n):
                    s = n % MX
                    p = n % 2
                    st = stt[p]
                    stn = "stt%d" % p
                    rms_stats(xs[s], "xs%d" % s, st, stn, None)
                    P.op("dve", lambda e: e.scalar_tensor_tensor(out=xn[0][:], in0=xs[s][:], scalar=st[:, 2:3], in1=gb[:, 0, :],
                                                                 op0=ALU.mult, op1=ALU.mult),
                         reads=["xs%d" % s, stn, "gb"], writes=["xn0"])

                def F1(n):
                    for c in range(8):
                        P.op("pe", lambda e, c=c: e.transpose(out=tp[:, c * 128:(c + 1) * 128], in_=xn[0][:, c * 128:(c + 1) * 128],
                                                              identity=ident[:]),
                             reads=["xn0", "ident"], writes=["tp"], accum=(c > 0), inc=(c == 7))
                    P.op("dve", lambda e: e.tensor_copy(out=xnT[0][:].rearrange("p c t -> p (c t)"), in_=tp[:]),
                         reads=["tp"], writes=["xnT0"])

                def F2(n):
                    i = order[n]
                    r0, r1 = i * 128, (i + 1) * 128
                    p = n % 2
                    kbt, kbn = KB(n)
                    b = proj_piece(0, 0, 512)
                    P.op("act", lambda e, b=b: e.activation(out=qb[:], in_=mm[b][:], func=AF.Copy),
                         reads=["mm%d" % b], writes=["qb"])
                    b = proj_piece(0, 512, 1024)
                    P.op("act", lambda e, b=b: e.activation(out=kbt[:], in_=mm[b][:], func=AF.Copy, scale=128.0 ** -0.5),
                         reads=["mm%d" % b], writes=[kbn])
                    P.op("sp", lambda e: e.dma_start(out=s_kb[r0:r1, :], in_=kbt[:]),
                         reads=[kbn], writes=[("s_kb", i)], dma=kbn)
                    for h2 in range(2):
                        b = proj_piece(0, 1024 + h2 * 512, 1536 + h2 * 512)
                        P.op("act", lambda e, b=b, h2=h2: e.activation(out=vf[:, 2 * h2:2 * h2 + 2, 0:256],
                                                                       in_=mm[b][:].rearrange("p (h d) -> p h d", h=2),
                                                                       func=AF.Copy),
                             reads=["mm%d" % b], writes=["vf"], accum=(h2 > 0))
                        P.op("act", lambda e, b=b, h2=h2: e.activation(out=qr[:, h2 * 512:(h2 + 1) * 512], in_=mm[b][:], func=AF.Copy),
                             reads=["mm%d" % b], writes=["qr"], accum=(h2 > 0))
                    P.op("sp", lambda e: e.dma_start(out=s_v[r0:r1, :], in_=qr[:]),
                         reads=["qr"], writes=[("s_v", i)], dma="qr")
                    b = proj_piece(0, 4096, 4112)
                    P.op("dve", lambda e, b=b: e.tensor_tensor(out=gt[:], in0=mm[b][:, 0:16], in1=gbias[:, j, :], op=ALU.add),
                         reads=["mm%d" % b, "gbias"], writes=["gt"])
                    P.op("sp", lambda e: e.dma_start(out=s_gt[r0:r1, :], in_=gt[:]),
                         reads=["gt"], writes=[("s_gt", i)], dma="gt")
                    gsm = gsms[p]
                    gs = "_%d" % p
                    P.op("act", lambda e: e.activation(out=gsm[:, 0, :], in_=gt[:, fg0:fg0 + 4], func=AF.Exp, scale=-1.0),
                         reads=["gt"], writes=["g_e1" + gs])
                    P.op("act", lambda e: e.activation(out=gsm[:, 1, :], in_=gsm[:, 0, :], func=AF.Ln, bias=1.0),
                         reads=["g_e1" + gs], writes=["g_l" + gs])
                    for h2 in range(2):
                        b = proj_piece(0, 2048 + h2 * 512, 2560 + h2 * 512)
                        P.op("act", lambda e, b=b, h2=h2: e.activation(out=so[:, h2 * 512:(h2 + 1) * 512], in_=mm[b][:],
                                                                       func=AF.Sigmoid),
                             reads=["mm%d" % b], writes=["so"], accum=(h2 > 0))
                    for h2 in range(2):
                        b = proj_piece(0, 3072 + h2 * 512, 3584 + h2 * 512)
                        P.op("act", lambda e, b=b, h2=h2: e.activation(out=szm[:, h2 * 512:(h2 + 1) * 512], in_=mm[b][:],
                                                                       func=AF.Silu),
                             reads=["mm%d" % b], writes=["szm"], accum=(h2 > 0))
                    P.op("pool", lambda e: e.tensor_tensor(out=szm[:], in0=so[:], in1=szm[:], op=ALU.mult),
                         reads=["so", "szm"], writes=["szm"])
                    P.op("pool", lambda e: e.tensor_tensor(out=gzb[p][:], in0=szm[:], in1=hng[:], op=ALU.mult),
                         reads=["szm", "hng"], writes=["gzb%d" % p])
                    P.op("sp", lambda e: e.dma_start(out=s_gz[r0:r1, :], in_=gzb[p][:]),
                         reads=["gzb%d" % p], writes=[("s_gz", i)], dma="gzb%d" % p)

                def F3a(n):
                    i = order[n]
                    r0, r1 = i * 128, (i + 1) * 128
                    p = n % 2
                    gsm = gsms[p]
                    gs = "_%d" % p
                    rbcol = i if dirf else NBLK + i
                    gtt, gtn = GT(n)
                    if A:
                        kbt, kbn = KB(n)
                        for c in range(8):
                            srct = qb if c < 4 else kbt
                            P.op("pe", lambda e, c=c, srct=srct: e.transpose(out=tp[:, c * 128:(c + 1) * 128],
                                                                              in_=srct[:, (c % 4) * 128:(c % 4 + 1) * 128], identity=ident[:]),
                                 reads=["qb", kbn, "ident"], writes=["tp"], accum=(c > 0), inc=(c == 7))
                        P.op("dve", lambda e: e.tensor_copy(out=qkT[:].rearrange("p c t -> p (c t)"), in_=tp[:]),
                             reads=["tp"], writes=["qkT"])
                        P.op("sp", lambda e: e.dma_start(out=s_qk[r0:r1, :], in_=qkT[:].rearrange("p c t -> p (c t)")),
                             reads=["qkT"], writes=[("s_qk", i)], dma="qkT")
                    else:
                        P.op("act", lambda e: e.activation(out=gsm[:, 0, :], in_=gtt[:, fg0:fg0 + 4], func=AF.Exp, scale=-1.0),
                             reads=[gtn], writes=["g_e1" + gs])
                        P.op("act", lambda e: e.activation(out=gsm[:, 1, :], in_=gsm[:, 0, :], func=AF.Ln, bias=1.0),
                             reads=["g_e1" + gs], writes=["g_l" + gs])
                    P.op("pe", lambda e: e.matmul(sm[:, 0:4], lhsT=Mx[:], rhs=gsm[:, 1, :], start=True, stop=True),
                         reads=[Mn, "g_l" + gs], writes=["sm"])
                    P.op("pe", lambda e: e.matmul(sm[:, 4:8], lhsT=ones[:], rhs=gsm[:, 1, :], start=True, stop=True),
                         reads=["ones", "g_l" + gs], writes=["sm"], accum=True)
                    P.op("dve", lambda e: e.tensor_tensor(out=gsm[:, 2, :], in0=gtt[:, ig0:ig0 + 4], in1=sm[:, 0:4], op=ALU.subtract),
                         reads=[gtn, "sm"], writes=["g_ta" + gs])
                    P.op("act", lambda e: e.activation(out=gsm[:, 3, :], in_=gsm[:, 2, :], func=AF.Exp),
                         reads=["g_ta" + gs], writes=["g_ea" + gs])
                    P.op("act", lambda e: e.activation(out=gsm[:, 4, :], in_=sm[:, 0:4], func=AF.Exp),
                         reads=["sm"], writes=["g_eo" + gs])
                    P.op("act", lambda e: e.activation(out=gsm[:, 6, :], in_=sm[:, 0:4], func=AF.Exp, scale=-1.0),
                         reads=["sm"], writes=["g_ei" + gs])
                    P.op("act", lambda e: e.activation(out=gsm[:, 5, :], in_=sm[:, 4:8], func=AF.Exp, scale=-1.0,
                                                       bias=rb[:, rbcol:rbcol + 1]),
                         reads=["sm", "rb"], writes=["g_dec" + gs])

                def F3b(n):
                    p = n % 2
                    gsm = gsms[p]
                    gs = "_%d" % p
                    if A:
                        P.op("dve", lambda e: e.tensor_tensor(out=vt[:], in0=vf[:],
                                                              in1=gsm[:, 3, :].unsqueeze(2).to_broadcast([128, 4, 260]), op=ALU.mult),
                             reads=["vf", "g_ea" + gs], writes=["vt"])
                    else:
                        vb, vbn = v3[n % 3]
                        P.op("dve", lambda e: e.tensor_tensor(out=vt[:, :, 0:256], in0=vb.rearrange("p (h d) -> p h d", h=4),
                                                              in1=gsm[:, 3, :].unsqueeze(2).to_broadcast([128, 4, 256]), op=ALU.mult),
                             reads=[vbn, "g_ea" + gs], writes=["vt"])
                        P.op("dve", lambda e: e.tensor_copy(out=vt[:, :, 256:260],
                                                            in_=gsm[:, 3, :].unsqueeze(2).to_broadcast([128, 4, 4])),
                             reads=["g_ea" + gs], writes=["vt"], accum=True)

                def F4(n):
                    qk, qkn = QK(n)
                    sb_ = stc[0] % 2
                    stc[0] += 1
                    for h in range(4):
                        P.op("pe", lambda e, h=h: e.matmul(stp[sb_][:, h * 128:(h + 1) * 128], lhsT=qk[:, 4 + h, :], rhs=qk[:, h, :],
                                                           start=True, stop=True),
                             reads=[qkn], writes=["st%d" % sb_], accum=(h > 0), inc=(h == 3))
                    P.op("dve", lambda e: e.tensor_tensor(out=PT[0][:], in0=stp[sb_][:], in1=mk[:], op=ALU.mult),
                         reads=["st%d" % sb_, mkn], writes=["PT0"])

                def B1a(n):
                    gsm = gsms[n % 2]
                    gs = "_%d" % (n % 2)
                    for h in range(4):
                        P.op("act", lambda e, h=h: e.activation(out=cbf[:, h, :], in_=Cst[:, h, :], func=AF.Copy, scale=gsm[:, 5, h:h + 1]),
                             reads=["Cst", "g_dec" + gs], writes=["cbf"], accum=(h > 0))

                def B1b(n):
                    qk, qkn = QK(n)
                    for h in range(4):
                        P.op("pe", lambda e, h=h: e.matmul(pv[:, h * 256:(h + 1) * 256], lhsT=PT[0][:, h * 128:(h + 1) * 128],
                                                           rhs=vt[:, h, 0:256], start=True, stop=False),
                             reads=["PT0", "vt"], writes=["pv"], accum=(h > 0), inc=False)
                        P.op("pe", lambda e, h=h: e.matmul(pv[:, h * 256:(h + 1) * 256], lhsT=qk[:, h, :],
                                                           rhs=cbf[:, h, 0:256], start=False, stop=True),
                             reads=[qkn, "cbf"], writes=["pv"], accum=True, inc=False)
                        P.op("pe", lambda e, h=h: e.matmul(sm[:, 16 + 2 * h:18 + 2 * h], lhsT=PT[0][:, h * 128:(h + 1) * 128],
                                                           rhs=vt[:, h, 256:258], start=True, stop=False),
                             reads=["PT0", "vt"], writes=["sm"], accum=True, inc=False)
                        P.op("pe", lambda e, h=h: e.matmul(sm[:, 16 + 2 * h:18 + 2 * h], lhsT=qk[:, h, :],
                                                           rhs=cbf[:, h, 256:258], start=False, stop=True),
                             reads=[qkn, "cbf"], writes=["sm"], accum=True, inc=(h == 3))

                def B2(n):
                    i = order[n]
                    p = n % 2
                    gsm = gsms[p]
                    gs = "_%d" % p
                    kbt, kbn = KB(n)
                    for h in range(4):
                        sb2 = stc[0] % 2
                        stc[0] += 1
                        P.op("pe", lambda e, h=h, sb2=sb2: e.matmul(stp[sb2][:, 0:258], lhsT=kbt[:, h * 128:(h + 1) * 128], rhs=vt[:, h, 0:258],
                                                                    start=True, stop=True),
                             reads=[kbn, "vt"], writes=["st%d" % sb2])
                        P.op("dve", lambda e, h=h, sb2=sb2: e.scalar_tensor_tensor(out=Cst[:, h, 0:258], in0=Cst[:, h, 0:258],
                                                                                   scalar=gsm[:, 5, h:h + 1],
                                                                                   in1=stp[sb2][:, 0:258], op0=ALU.mult, op1=ALU.add),
                             reads=["Cst", "g_dec" + gs, "st%d" % sb2], writes=["Cst"], accum=(h > 0))
                    den4 = sm[:, 16:24].rearrange("p (h t) -> p h t", t=2)[:, :, 0]
                    P.op("dve", lambda e: e.tensor_tensor(out=osm[:, 0, :], in0=den4, in1=gsm[:, 6, :], op=ALU.max),
                         reads=["sm", "g_ei" + gs], writes=["o_a1"])
                    P.op("dve", lambda e: e.scalar_tensor_tensor(out=osm[:, 1, :], in0=den4, scalar=-1.0, in1=osm[:, 0, :],
                                                                 op0=ALU.mult, op1=ALU.max),
                         reads=["sm", "o_a1"], writes=["o_a2"])
                    P.op("dve", lambda e: e.reciprocal(out=osm[:, 4, :], in_=osm[:, 1, :]), reads=["o_a2"], writes=["o_hs"])
                    pv3 = pv[:].rearrange("p (h d) -> p h d", h=4)
                    hsb = osm[:, 4, :].unsqueeze(2).to_broadcast([128, 4, 256])
                    if A:
                        hb_ = n % 2
                        P.op("dve", lambda e: e.tensor_tensor(out=hbt[hb_][:].rearrange("p (h d) -> p h d", h=4), in0=pv3, in1=hsb, op=ALU.mult),
                             reads=["pv", "o_hs"], writes=["hbt%d" % hb_])
                        P.op("sp", lambda e: e.dma_start(out=hB[i * 128:(i + 1) * 128, :], in_=hbt[hb_][:]),
                             reads=["hbt%d" % hb_], writes=[("hB", i)], dma="hbt%d" % hb_)
                    else:
                        hs_ = n % 2
                        for h in range(4):
                            P.op("dve", lambda e, h=h: e.scalar_tensor_tensor(out=hsum[:, h * 256:(h + 1) * 256], in0=pv[:, h * 256:(h + 1) * 256],
                                                                              scalar=osm[:, 4, h:h + 1], in1=hbl[hs_][:, h * 256:(h + 1) * 256],
                                                                              op0=ALU.mult, op1=ALU.add),
                                 reads=["pv", "o_hs", "hbl%d" % hs_], writes=["hsum"], accum=(h > 0))

                def B3v1(n):
                    hs_ = n % 2
                    for h in range(4):
                        P.op("act", lambda e, h=h: e.activation(out=junk[:, 0:256], in_=hsum[:, h * 256:(h + 1) * 256], func=AF.Square,
                                                                accum_out=osm[:, 5, h:h + 1]),
                             reads=["hsum"], writes=["o_hss", "junk"])
                    P.op("act", lambda e: e.activation(out=osm[:, 6, :], in_=osm[:, 5, :], func=AF.Ln, scale=1.0 / 256, bias=EPS),
                         reads=["o_hss"], writes=["o_hln"])
                    P.op("act", lambda e: e.activation(out=osm[:, 7, :], in_=osm[:, 6, :], func=AF.Exp, scale=-0.5),
                         reads=["o_hln"], writes=["o_hr"])

                def B3v2(n):
                    gzt, gzn = GZ(n)
                    P.op("dve", lambda e: e.tensor_tensor(out=hn[:].rearrange("p (h d) -> p h d", h=4),
                                                          in0=hsum[:].rearrange("p (h d) -> p h d", h=4),
                                                          in1=osm[:, 7, :].unsqueeze(2).to_broadcast([128, 4, 256]), op=ALU.mult),
                         reads=["hsum", "o_hr"], writes=["hn"])
                    P.op("pool", lambda e: e.tensor_tensor(out=og[:], in0=hn[:], in1=gzt[:], op=ALU.mult),
                         reads=["hn", gzn], writes=["og"])

                def B3pe(n):
                    bk = mmctr[0] % 2
                    mmctr[0] += 1
                    mmb = mm[bk][:].bitcast(BF16)
                    for c in range(8):
                        P.op("pe", lambda e, c=c: e.transpose(out=mmb[:, c * 128:(c + 1) * 128], in_=og[:, c * 128:(c + 1) * 128],
                                                              identity=ident[:]),
                             reads=["og", "ident"], writes=["mm%d" % bk], accum=(c > 0), inc=(c == 7))
                    P.op("dve", lambda e: e.tensor_copy(out=ogT[:].rearrange("p c t -> p (c t)"), in_=mmb),
                         reads=["mm%d" % bk], writes=["ogT"])

                def B4(n):
                    out_proj_and_store(order[n], n % MX, ogT, "ogT", dst, dst_tag, final_norm)

                P.op("pool", lambda e: e.memset(Cst[:].rearrange("p a b -> p (a b)"), 0.0), writes=["Cst"])
                if A:
                    for n in range(min(3, NBLK)):
                        L(n)
                    for n in range(-2, NBLK + 1):
                        if n + 3 >= 3 and ok(n + 3):
                            L(n + 3)
                        if ok(n + 1):
                            F1(n + 1)
                        if ok(n + 2):
                            F0(n + 2)
                        if ok(n + 1):
                            F2(n + 1)
                        if ok(n):
                            B1b(n)
                        if ok(n + 1):
                            F3a(n + 1)
                        if ok(n):
                            B2(n)
                        if ok(n + 1):
                            F4(n + 1)
                            B1a(n + 1)
                            F3b(n + 1)
                else:
                    for n in range(min(2, NBLK)):
                        LB(n)
                    LHB(0)
                    for n in range(-1, NBLK + 1):
                        if n + 2 >= 2 and ok(n + 2):
                            LB(n + 2)
                        if n + 1 >= 1 and ok(n + 1):
                            LHB(n + 1)
                        if ok(n):
                            B1b(n)
                            B2(n)
                        if ok(n + 1):
                            F3a(n + 1)
                        if ok(n - 1):
                            B3pe(n - 1)
                        if ok(n + 1):
                            F3b(n + 1)
                            F4(n + 1)
                            B1a(n + 1)
                        if ok(n):
                            B3v1(n)
                        if ok(n - 1):
                            B4(n - 1)
                        if ok(n):
                            B3v2(n)

            run_pass("A")
            run_pass("B")

        if dbg == 1:
            depth = 0
            P.op("sp", lambda e: e.dma_start(out=y[0:128, :], in_=gb[:, 1, :]), reads=["gbf"], writes=[("y", 0)], dma="dbg")
        if dbg == 2:
            depth = 0
            load_weight(w_in, attn_w_in[0], ATT_IN, "w_in")
            load_x(xin, 0, "xin", with_tab=True)
            P.op("sp", lambda e: e.dma_start(out=gb[:, 0, :], in_=norm_g[0, :].partition_broadcast(128)), writes=["gb"], dma="gb")
            norm_and_transpose(0, 0, None)
            b = proj_piece(0, 0, 512)
            P.op("act", lambda e: e.activation(out=xo[0][:, 0:512], in_=mm[b][:], func=AF.Copy), reads=["mm%d" % b], writes=["xo0"])
            P.op("sp", lambda e: e.dma_start(out=y[0:128, 0:512], in_=xo[0][:, 0:512]), reads=["xo0"], writes=[("y", 0)], dma="dbg")
        if dbg >= 3:
            depth = 1
        for layer in range(depth):
            src, src_tag = (xin, "xin") if layer == 0 else (xA, "xA")
            last = (layer == depth - 1)
            dst, dst_tag = (y, "y") if last else (xA, "xA")
            if layer % 2 == 0:
                attn_layer(layer, layer // 2, src, src_tag, dst, dst_tag, last)
            else:
                mlstm_layer(layer, layer // 2, src, src_tag, dst, dst_tag, last)

        with nc.Block() as block:
            @block.sync
            def _(e):
                P.emit("sp", e)
                P.final_waits(e)

            @block.scalar
            def _(e):
                P.emit("act", e)

            @block.vector
            def _(e):
                P.emit("dve", e)

            @block.gpsimd
            def _(e):
                P.emit("pool", e)

            @block.tensor
            def _(e):
                P.emit("pe", e)
        ninstr = {k: len(v) for k, v in P.q.items()}
    return nc, ninstr


ROPE_THETA = 10000.0


def _rope_tab(pos):
    half = 32
    inv = np.exp(np.float32(-math.log(ROPE_THETA)) * np.arange(half, dtype=np.float32) / np.float32(half)).astype(np.float32)
    ang = (pos.astype(np.float32)[:, None] * inv[None, :]).astype(np.float32)
    c = np.cos(ang.astype(np.float64)).astype(np.float32)
    s = np.sin(ang.astype(np.float64)).astype(np.float32)
    return np.concatenate([c, c, -s, s], axis=1).astype(np.float32)


def _core_inputs(seqs, NBLK, SLOT):
    NTOK = NBLK * 128
    x = np.zeros((NTOK, D), np.float32)
    whole = (len(seqs) == 1 and seqs[0].shape[0] == NTOK)
    if whole:
        x[:] = seqs[0]
        pos = np.arange(NTOK)
    else:
        L = SLOT * 128
        for n, sq in enumerate(seqs):
            assert sq.shape[0] == L
            x[n * L:(n + 1) * L] = sq
        pos = np.arange(NTOK) % L
    tab = _rope_tab(pos)
    rb = np.zeros((128, 2 * NBLK), np.float32)
    jj = np.arange(128)
    m_ge = (jj[:, None] >= jj[None, :]).astype(np.float32)
    m_le = (jj[:, None] <= jj[None, :]).astype(np.float32)
    bmask = np.zeros((128, 1024), np.float32)
    if whole:
        bmask[:, 0:512] = np.tile(m_ge, (1, 4))
        bmask[:, 512:1024] = np.tile(m_le, (1, 4))
    else:
        for i in range(NBLK):
            if i % SLOT == 0:
                rb[:, i] = NEG
            if (i + 1) % SLOT == 0:
                rb[:, NBLK + i] = NEG
    return {"x": x, "tab": tab, "rbias": rb, "bmask": bmask.astype(ml_dtypes.bfloat16)}


_CACHE = {}


def run_cores(core_seqs, weights, NBLK, SLOT, depth=4, dbg=0):
    key = (NBLK, SLOT, depth)
    if key not in _CACHE:
        _CACHE[key] = build_nc(NBLK, SLOT, depth, dbg)
    nc, _ = _CACHE[key]
    w = {k: np.ascontiguousarray(np.asarray(v, dtype=np.float32)) for k, v in weights.items()}
    w["final_norm_g"] = w["final_norm_g"].reshape(1, D)
    in_maps = []
    for seqs in core_seqs:
        m = _core_inputs(seqs, NBLK, SLOT)
        m.update(w)
        in_maps.append(m)
    res = run_bass_kernel_spmd(nc, in_maps, core_ids=list(range(len(core_seqs))))
    return [r["y"] for r in res.results]


def kernel(x_prompt, x_sample, norm_g, attn_w_in, attn_sink, attn_w_out, mlstm_w_in, mlstm_gate_bias,
           mlstm_head_norm, mlstm_w_out, final_norm_g):
    x_prompt = np.asarray(x_prompt, dtype=np.float32)
    x_sample = np.asarray(x_sample, dtype=np.float32)
    NBLK, SLOT = 128, 16
    weights = dict(norm_g=norm_g, attn_w_in=attn_w_in, attn_sink=attn_sink, attn_w_out=attn_w_out,
                   mlstm_w_in=mlstm_w_in, mlstm_gate_bias=mlstm_gate_bias, mlstm_head_norm=mlstm_head_norm,
                   mlstm_w_out=mlstm_w_out, final_norm_g=final_norm_g)
    counts = [6, 6, 5, 5, 5, 5]
    core_seqs = [[x_prompt[0]], [x_prompt[1]]]
    assign = []
    n0 = 0
    for c in counts:
        ids = list(range(n0, n0 + c))
        n0 += c
        assign.append(ids)
        seqs = [x_sample[k] for k in ids]
        while len(seqs) < 8:
            seqs.append(np.zeros((2048, D), np.float32))
        core_seqs.append(seqs)
    outs = run_cores(core_seqs, weights, NBLK, SLOT)
    y_prompt = np.stack([outs[0], outs[1]], axis=0).astype(np.float32)
    y_sample = np.empty((32, 2048, D), np.float32)
    for ci, ids in enumerate(assign):
        o = outs[2 + ci]
        for n, k in enumerate(ids):
            y_sample[k] = o[n * 2048:(n + 1) * 2048]
    return (y_prompt, y_sample)
```
